# Optimizing a Trainium2 kernel written in Bass

```python
import jax, jax.numpy as jnp
from jax import lax
import numpy as np

D_MODEL = 2048
BATCH = 2
SEQ = 8192
DEPTH = 1

N_MEM = 256
EPS = 1e-6
CONV_WIDTH = D_MODEL
CONV_K = 31
GLA_HEADS = 4
GLA_DK = D_MODEL // 2
GLA_DV = D_MODEL
GLA_DKH = GLA_DK // GLA_HEADS
GLA_DVH = GLA_DV // GLA_HEADS
GLA_RANK = 16
GLA_TAU = 16.0
CHUNK = 64
MEM_HEADS = 4
MEM_HD = 128
MEM_WIDTH = MEM_HEADS * MEM_HD
N_BRANCH = 3

SPLIT_SIZES = [
    CONV_WIDTH,
    CONV_WIDTH,
    CONV_WIDTH,
    GLA_DK,
    GLA_DK,
    GLA_DV,
    GLA_DV,
    GLA_RANK,
    MEM_WIDTH,
    MEM_WIDTH,
    N_BRANCH * D_MODEL,
]
SPLITS = [int(s) for s in np.cumsum(SPLIT_SIZES)[:-1]]
D_IN = int(sum(SPLIT_SIZES))

kernel_name = "hybrid_conv_gla_memxattn_gated_merge"


def rmsnorm(x, g):
    xf = x.astype(jnp.float32)
    y = xf * lax.rsqrt(jnp.mean(xf * xf, axis=-1, keepdims=True) + EPS)
    return (y * g.astype(jnp.float32)).astype(x.dtype)


def layernorm(x, g, b):
    xf = x.astype(jnp.float32)
    mu = jnp.mean(xf, axis=-1, keepdims=True)
    var = jnp.mean(jnp.square(xf - mu), axis=-1, keepdims=True)
    y = (xf - mu) * lax.rsqrt(var + EPS)
    return (y * g.astype(jnp.float32) + b.astype(jnp.float32)).astype(x.dtype)


def causal_depthwise_conv(u, w, b):
    C = u.shape[-1]
    y = lax.conv_general_dilated(
        u, w[:, None, :].astype(u.dtype), window_strides=(1,),
        padding=[(CONV_K - 1, 0)],
        dimension_numbers=("NWC", "WIO", "NWC"),
        feature_group_count=C)
    return y + b.astype(u.dtype)


def gla_chunked(q, k, v, log_a):
    B, S, H, dk = q.shape
    dv = v.shape[-1]
    N = S // CHUNK

    def to_chunks(t):
        return t.astype(jnp.float32).reshape(B, N, CHUNK, H, t.shape[-1]).transpose(1, 0, 3, 2, 4)

    qc, kc, vc = to_chunks(q), to_chunks(k), to_chunks(v)
    bc = jnp.cumsum(to_chunks(log_a), axis=-2)
    mask = jnp.tril(jnp.ones((CHUNK, CHUNK), dtype=bool))[:, :, None]

    def step(state, inp):
        qi, ki, vi, bi = inp
        o_inter = jnp.einsum("bhtk,bhkv->bhtv", qi * jnp.exp(bi), state)
        diff = bi[:, :, :, None, :] - bi[:, :, None, :, :]
        decay = jnp.exp(jnp.where(mask, diff, -jnp.inf))
        scores = jnp.einsum("bhtk,bhsk,bhtsk->bhts", qi, ki, decay)
        o_intra = jnp.einsum("bhts,bhsv->bhtv", scores, vi)
        b_last = bi[:, :, -1, :]
        k_dec = ki * jnp.exp(b_last[:, :, None, :] - bi)
        state = jnp.exp(b_last)[..., None] * state + jnp.einsum("bhck,bhcv->bhkv", k_dec, vi)
        return state, o_inter + o_intra

    s0 = jnp.zeros((B, H, dk, dv), jnp.float32)
    _, o = lax.scan(step, s0, (qc, kc, vc, bc))
    return o.transpose(1, 0, 3, 2, 4).reshape(B, S, H, dv)


def setup_inputs(seed: int = 0) -> dict:
    key = jax.random.key(seed)
    ks = jax.random.split(key, 24)
    f32 = jnp.float32
    L = DEPTH

    def nrm(k, shape, scale):
        return jax.random.normal(k, shape, f32) * scale

    return {
        "x": jax.random.normal(ks[0], (BATCH, SEQ, D_MODEL), f32),
        "mem": jax.random.normal(ks[1], (BATCH, N_MEM, D_MODEL), f32),
        "ln_in_g": 1.0 + nrm(ks[2], (L, D_MODEL), 0.02),
        "mem_ln_g": 1.0 + nrm(ks[3], (L, D_MODEL), 0.02),
        "w_in": nrm(ks[4], (L, D_MODEL, D_IN), D_MODEL ** -0.5),
        "b_gate": nrm(ks[5], (L, N_BRANCH * D_MODEL), 0.02),
        "dw_w": nrm(ks[6], (L, CONV_K, CONV_WIDTH), CONV_K ** -0.5),
        "dw_b": nrm(ks[7], (L, CONV_WIDTH), 0.02),
        "conv_ln_g": 1.0 + nrm(ks[8], (L, CONV_WIDTH), 0.02),
        "conv_ln_b": nrm(ks[9], (L, CONV_WIDTH), 0.02),
        "w_conv_out": nrm(ks[10], (L, CONV_WIDTH, D_MODEL), CONV_WIDTH ** -0.5),
        "b_conv_out": nrm(ks[11], (L, D_MODEL), 0.02),
        "w_alpha2": nrm(ks[12], (L, GLA_RANK, GLA_DK), GLA_RANK ** -0.5),
        "b_alpha": nrm(ks[13], (L, GLA_DK), 0.02),
        "gla_norm_g": 1.0 + nrm(ks[14], (L, GLA_DV), 0.02),
        "w_gla_out": nrm(ks[15], (L, GLA_DV, D_MODEL), GLA_DV ** -0.5),
        "w_mem_kv": nrm(ks[16], (L, D_MODEL, 2 * MEM_WIDTH), D_MODEL ** -0.5),
        "w_mem_out": nrm(ks[17], (L, MEM_WIDTH, D_MODEL), MEM_WIDTH ** -0.5),
        "w_out": nrm(ks[18], (L, D_MODEL, D_MODEL), D_MODEL ** -0.5),
        "final_g": 1.0 + nrm(ks[19], (D_MODEL,), 0.02),
    }


def reference(x, mem, ln_in_g, mem_ln_g, w_in, b_gate, dw_w, dw_b, conv_ln_g, conv_ln_b,
              w_conv_out, b_conv_out, w_alpha2, b_alpha, gla_norm_g, w_gla_out,
              w_mem_kv, w_mem_out, w_out, final_g):
    B, S, D = x.shape
    M = mem.shape[1]
    for l in range(DEPTH):
        h = rmsnorm(x, ln_in_g[l])
        proj = h @ w_in[l]
        (conv_a, conv_b, conv_z, q, k, v, gla_z, alpha_lr, mq, mz, gate_pre) = jnp.split(proj, SPLITS, axis=-1)

        u = conv_a * jax.nn.sigmoid(conv_b)
        u = causal_depthwise_conv(u, dw_w[l], dw_b[l])
        u = jax.nn.silu(layernorm(u, conv_ln_g[l], conv_ln_b[l]))
        u = u * jax.nn.silu(conv_z)
        y_conv = u @ w_conv_out[l] + b_conv_out[l]

        log_a = jax.nn.log_sigmoid((alpha_lr @ w_alpha2[l] + b_alpha[l]).astype(jnp.float32)) / GLA_TAU
        qh = q.reshape(B, S, GLA_HEADS, GLA_DKH) * (GLA_DKH ** -0.5)
        kh = k.reshape(B, S, GLA_HEADS, GLA_DKH)
        vh = v.reshape(B, S, GLA_HEADS, GLA_DVH)
        o = gla_chunked(qh, kh, vh, log_a.reshape(B, S, GLA_HEADS, GLA_DKH)).astype(x.dtype)
        o = rmsnorm(o, gla_norm_g[l].reshape(GLA_HEADS, GLA_DVH)).reshape(B, S, GLA_DV)
        y_gla = (o * jax.nn.silu(gla_z)) @ w_gla_out[l]

        m = rmsnorm(mem, mem_ln_g[l])
        mk, mv = jnp.split(m @ w_mem_kv[l], 2, axis=-1)
        mk = mk.reshape(B, M, MEM_HEADS, MEM_HD)
        mv = mv.reshape(B, M, MEM_HEADS, MEM_HD)
        mqh = mq.reshape(B, S, MEM_HEADS, MEM_HD)
        sc = jnp.einsum("bshd,bmhd->bhsm", mqh.astype(jnp.float32), mk.astype(jnp.float32)) * (MEM_HD ** -0.5)
        p = jax.nn.softmax(sc, axis=-1).astype(x.dtype)
        om = jnp.einsum("bhsm,bmhd->bshd", p, mv).reshape(B, S, MEM_WIDTH)
        y_mem = (om * jax.nn.silu(mz)) @ w_mem_out[l]

        g = jax.nn.sigmoid(gate_pre + b_gate[l]).reshape(B, S, N_BRANCH, D)
        y = g[:, :, 0] * y_conv + g[:, :, 1] * y_gla + g[:, :, 2] * y_mem
        x = x + y @ w_out[l]
    return rmsnorm(x, final_g)
```

```python
import contextlib
import numpy as np
import concourse.bass as bass
import concourse.mybir as mybir
from concourse.bass_utils import run_bass_kernel_spmd

F32 = mybir.dt.float32
BF16 = mybir.dt.bfloat16
U8 = mybir.dt.uint8
AF = mybir.ActivationFunctionType
ALU = mybir.AluOpType
AX = mybir.AxisListType

D = 2048
KC = 16
G = 512
NT = 4
EPS = 1e-6
N_CORES = 8
NG_FULL = 4
NPG_FULL = 12

C_A, C_B, C_Z, C_Q, C_K, C_V, C_GZ, C_AL, C_MQ, C_MZ, C_GATE = 0, 2048, 4096, 6144, 7168, 8192, 10240, 12288, 12304, 12816, 13328
B1_A, B1_B, B1_Z, B1_Q, B1_K, B1_MQ, B1_MZ, B1_G = 0, 16, 32, 48, 56, 64, 68, 72
NB1 = 120
B2_K, B2_V, B2_Z = 0, 4, 12
NB2 = 20
V_BG, V_DWB, V_LNG, V_LNB, V_ING, V_GNG, V_MLG, NV = 0, 48, 64, 80, 96, 112, 128, 144


class Buf:
    def __init__(self, t, name=''):
        self.t = t
        self.name = name
        self.lw = None
        self.rd = {}

    def __getitem__(self, idx):
        return self.t[idx]


class Sched:
    ENG = ['pe', 'act', 'dve', 'pool', 'sp']

    def __init__(self, nc, st, same_sync=('act', 'dve', 'pool')):
        self.nc = nc
        self.st = st
        self.sem = {n: st.enter_context(nc.semaphore('s_' + n)) for n in self.ENG}
        self.dsem_h = {}
        self.same_sync = set(same_sync)
        self.nsb = 0
        self.dry = False
        self.reset()

    def reset(self):
        self.prog = {n: [] for n in self.ENG}
        self.cnt = {n: 0 for n in self.ENG}
        self.waited = {n: {} for n in self.ENG}
        self.dcnt = {}
        self.pend = {n: False for n in self.ENG}
        self.bufs = []

    def track(self, b):
        self.bufs.append(b)
        return b

    def sb(self, shape, dt, name=None):
        self.nsb += 1
        name = name or ('sb%d' % self.nsb)
        return Buf(self.st.enter_context(self.nc.sbuf_tensor(name, list(shape), dt)), name)

    def ps(self, shape, dt, name=None):
        self.nsb += 1
        name = name or ('ps%d' % self.nsb)
        return Buf(self.st.enter_context(self.nc.psum_tensor(name, list(shape), dt)), name)

    def dma_sem(self, name):
        if name not in self.dsem_h:
            self.dsem_h[name] = self.st.enter_context(self.nc.semaphore('d_' + name))
        if name not in self.dcnt:
            self.dcnt[name] = 0
        return self.dsem_h[name]

    def _wait(self, eng, evs):
        w = self.waited[eng]
        need = {}
        for ev in evs:
            if ev is None:
                continue
            s, v, src = ev
            if src == eng and eng not in self.same_sync:
                continue
            k = id(s)
            if w.get(k, 0) >= v:
                continue
            if k not in need or need[k][1] < v:
                need[k] = (s, v)
        for k, (s, v) in need.items():
            self.prog[eng].append(('wait', s, v))
            w[k] = v

    def _deps(self, reads, writes):
        evs = []
        for b in reads:
            evs.append(b.lw)
        for b in writes:
            evs.append(b.lw)
            evs.extend(b.rd.values())
        return evs

    def _mark(self, ev, reads, writes):
        k = id(ev[0])
        for b in reads:
            if k not in b.rd or b.rd[k][1] < ev[1]:
                b.rd[k] = ev
        for b in writes:
            b.lw = ev
            b.rd = {}

    def op(self, eng, meth, kw, reads=(), writes=(), signal=True):
        if self.dry:
            return
        self._wait(eng, self._deps(reads, writes))
        ev = (self.sem[eng], self.cnt[eng] + 1, eng)
        if signal:
            self.cnt[eng] += 1
            self.pend[eng] = False
        else:
            self.pend[eng] = True
        self.prog[eng].append(('inst', meth, kw, self.sem[eng] if signal else None, 1))
        self._mark(ev, reads, writes)

    def dma(self, eng, semname, out, in_, reads=(), writes=()):
        if self.dry:
            return
        self._wait(eng, self._deps(reads, writes))
        s = self.dma_sem(semname)
        self.dcnt[semname] += 16
        ev = (s, self.dcnt[semname], 'dma')
        self.prog[eng].append(('inst', 'dma_start', dict(out=out, in_=in_), s, 16))
        self._mark(ev, reads, writes)

    def dma_mark(self, eng, semname, out, in_, reads, markbuf):
        if self.dry:
            return
        self._wait(eng, self._deps(reads, []))
        s = self.dma_sem(semname)
        self.dcnt[semname] += 16
        ev = (s, self.dcnt[semname], 'dma')
        self.prog[eng].append(('inst', 'dma_start', dict(out=out, in_=in_), s, 16))
        self._mark(ev, reads, [])
        markbuf.lw = ev
        markbuf.rd = {}

    def dma_nw(self, eng, semname, out, in_, buf):
        if self.dry:
            return
        s = self.dma_sem(semname)
        self.dcnt[semname] += 16
        self.prog[eng].append(('inst', 'dma_start', dict(out=out, in_=in_), s, 16))
        buf.lw = (s, self.dcnt[semname], 'dma')
        buf.rd = {}

    def barrier(self):
        if self.dry:
            return
        for e in self.ENG:
            assert not self.pend[e]
        evs = [(self.sem[f], self.cnt[f], f) for f in self.ENG]
        evs += [(self.dsem_h[n], c, 'dma') for n, c in self.dcnt.items()]
        for e in self.ENG:
            self._wait(e, [ev for ev in evs if ev[2] != e and ev[1] > 0])

    def final_wait(self, eng, semname):
        self.prog[eng].append(('wait', self.dsem_h[semname], self.dcnt[semname]))

    def emit(self):
        nc = self.nc
        block = self.st.enter_context(nc.Block())

        def replay(name, eng):
            for it in self.prog[name]:
                if it[0] == 'wait':
                    eng.wait_ge(it[1], it[2])
                else:
                    ins = getattr(eng, it[1])(**it[2])
                    if it[3] is not None:
                        ins.then_inc(it[3], it[4])

        @block.tensor
        def _(e):
            replay('pe', nc.tensor)

        @block.scalar
        def _(e):
            replay('act', nc.scalar)

        @block.vector
        def _(e):
            replay('dve', nc.vector)

        @block.gpsimd
        def _(e):
            replay('pool', nc.gpsimd)

        @block.sync
        def _(e):
            replay('sp', nc.sync)


class WPool:
    def __init__(self, S, n, shape, dt, name, eng):
        self.S = S
        self.n = n
        self.name = name
        self.eng = eng
        self.bufs = [S.sb(shape, dt, '%s_%d' % (name, i)) for i in range(n)]
        self.seq = []
        self.pos = 0
        self.issued = 0

    def start_real(self):
        self.pos = 0
        self.issued = 0
        for b in self.bufs:
            b.lw = None
            b.rd = {}

    def get(self, parts, reads=()):
        S = self.S
        if S.dry:
            self.seq.append((parts, reads))
            self.pos += 1
            return self.bufs[(self.pos - 1) % self.n]
        idx = self.pos
        lim = min(len(self.seq), idx + self.n)
        while self.issued < lim:
            j = self.issued
            b = self.bufs[j % self.n]
            p, r = self.seq[j]
            if any(rb.lw is None for rb in r):
                assert j > idx, 'weight source not ready'
                break
            for (dstf, src) in p:
                S.dma(self.eng, '%s_%d' % (self.name, j % self.n), dstf(b), src, reads=list(r), writes=[b])
            self.issued += 1
        self.pos += 1
        return self.bufs[idx % self.n]


def build(nc, NG, NPG, dbg=False):
    NTOK = NG * G
    NPRE = NPG * G

    def dram(name, shape, dt=F32, kind="ExternalInput"):
        return nc.dram_tensor(name, list(shape), dt, kind=kind).ap()

    xm = dram("xm", [NTOK, D])
    xp = dram("xp", [NPRE, D])
    memd = dram("mem", [256, D])
    class DSrc:
        def __init__(self, name, shape, perblock=False):
            self.f = dram(name, shape)
            self.b = dram(name + "_b", shape, BF16, kind="Internal")
            self.buf = Buf(self.b, name)
            self.perblock = perblock
            self.bufs = [Buf(self.b, '%s_%d' % (name, i)) for i in range(shape[0])] if perblock else None

        def __getitem__(self, i):
            if self.perblock:
                return (self.b[i], self.bufs[i])
            return (self.b[i], self.buf)

    w1 = DSrc("w1", [NB1, 128, KC, 128])
    w2 = DSrc("w2", [NB2, 128, KC, 256], perblock=True)
    wal = dram("wal", [128, KC, 16])
    wco = DSrc("wco", [16, 128, 16, 128])
    wgo = DSrc("wgo", [16, 128, 16, 128])
    wmo = DSrc("wmo", [16, 128, 4, 128])
    wout = DSrc("wout", [8, 128, 16, 256])
    wmk = DSrc("wmk", [4, 128, KC, 128])
    wmv = DSrc("wmv", [2, 128, KC, 256])
    DSRCS = [w2, w1, wco, wgo, wmo, wout, wmk, wmv]
    cf = dram("cf", [128, 256])
    cbd = dram("cb", [128, 512])
    vecd = dram("vec", [128, NV])
    dwwd = dram("dww", [128, 16, 31])
    wa2d = dram("wa2", [33, 1024])
    bcod = DSrc("bco", [16, 1, 128])
    DSRCS.append(bcod)
    fgd = dram("fg", [D])
    outd = dram("out", [NTOK, D], kind="ExternalOutput")
    dgs = dram("dgs", [16, 2, 128, 16, 128], BF16, kind="Internal")
    if dbg:
        dbgd = dram("dbg", [16, 128, 512], kind="ExternalOutput")

    st = contextlib.ExitStack()
    S = Sched(nc, st)
    CF = S.sb([128, 256], F32, 'CF')
    CB = S.sb([128, 512], BF16, 'CB')
    VEC = S.sb([128, NV], F32, 'VEC')
    HB = S.sb([128, 48], F32, 'HB')
    HW = S.sb([128, 16, 31], F32, 'HW')
    FG = S.sb([128, D], F32, 'FG')
    WA2 = S.sb([33, 1024], BF16, 'WA2')
    ONES1 = S.sb([1, 512], BF16, 'ONES1')
    WAL = S.sb([128, KC, 16], BF16, 'WAL')
    HT = S.sb([128, KC, G], BF16, 'HT')
    HTH = S.sb([128, KC, 32], BF16, 'HTH')
    XB = [S.sb([128, D], F32, 'XB%d' % i) for i in range(2)]
    S32 = [S.sb([128, 512], F32, 'S32_%d' % i) for i in range(8)]
    SBF = [S.sb([128, 512], BF16, 'SBF_%d' % i) for i in range(8)]
    UH = S.sb([128, 16, 32], BF16, 'UH')
    MKT = S.sb([128, 4, 256], BF16, 'MKT')
    MV = S.sb([128, 2, 512], BF16, 'MV')
    ALT = S.sb([33, G], BF16, 'ALT')
    AL = S.sb([128, NT, 8], F32, 'AL')
    STT = [S.sb([128, 8], F32, 'ST%d' % i) for i in range(4)]
    DGD = Buf(dgs, 'dgs')
    W1 = WPool(S, 4, [128, 17, 128], BF16, 'W1', 'sp')
    W2 = WPool(S, 3, [128, KC, 256], BF16, 'W2', 'sp')
    DG = WPool(S, 2, [128, 16, 128], BF16, 'DG', 'sp')
    ARENA_BYTES = 76 * 1024
    arena = st.enter_context(nc.sbuf_tensor('arena', [128, ARENA_BYTES], U8))
    PSB = [S.ps([128, 512], F32, 'PS%d' % i) for i in range(8)]

    ident_f = CF[:, 0:128]
    mask_f = CF[:, 128:256]
    ident_b = CB[:, 0:128]
    tri_inc = CB[:, 128:256]
    tri_dec = CB[:, 256:384]
    ones_b = CB[:, 384:512]
    negcol = CB[:, 255:256]

    state = {'bank': 0, 'st': 0, 'apos': 0, 'evac': 0}

    def bank():
        b = PSB[state['bank'] % 6]
        state['bank'] += 1
        return b

    def stt_buf():
        b = STT[state['st'] % 4]
        state['st'] += 1
        return b

    def arena_reset(off=0):
        state['apos'] = off

    def carve(free_shape, dt, name='a'):
        esz = 4 if dt == F32 else 2
        n = int(np.prod(free_shape))
        nbytes = (n * esz + 31) // 32 * 32
        off = state['apos']
        assert off + nbytes <= ARENA_BYTES, (name, off, nbytes)
        state['apos'] = off + nbytes
        v = arena[:, off:off + n * esz].bitcast(dt)
        if len(free_shape) == 2:
            v = v.rearrange("p (a b) -> p a b", a=free_shape[0])
        elif len(free_shape) == 3:
            v = v.rearrange("p (a b c) -> p a b c", a=free_shape[0], b=free_shape[1])
        return Buf(v, name)

    def act(out, in_, func, R, Wr, **kw):
        S.op('act', 'activation', dict(out=out, in_=in_, func=func, **kw), R, Wr)

    def mm(out, lhsT, rhs, start, stop, R, Wr, sig=None):
        S.op('pe', 'matmul', dict(out=out, lhsT=lhsT, rhs=rhs, start=bool(start), stop=bool(stop)), R, Wr,
             signal=bool(stop) if sig is None else sig)

    def tr(out, in_, ident, R, Wr):
        S.op('pe', 'transpose', dict(out=out, in_=in_, identity=ident), R, Wr)

    def stt(out, in0, scalar, in1, op0, op1, R, Wr):
        S.op('dve', 'scalar_tensor_tensor', dict(out=out, in0=in0, scalar=scalar, in1=in1, op0=op0, op1=op1), R, Wr)

    def ts(out, in0, s1, op0, R, Wr, s2=None, op1=None, eng='dve'):
        kw = dict(out=out, in0=in0, scalar1=s1, scalar2=s2, op0=op0)
        if op1 is not None:
            kw['op1'] = op1
        S.op(eng, 'tensor_scalar', kw, R, Wr)

    def tt(out, in0, in1, op, R, Wr, eng='dve'):
        S.op(eng, 'tensor_tensor', dict(out=out, in0=in0, in1=in1, op=op), R, Wr)

    def cp(out, in_, R, Wr, eng='dve'):
        if eng == 'act':
            act(out, in_, AF.Copy, R, Wr)
        else:
            S.op(eng, 'tensor_copy', dict(out=out, in_=in_), R, Wr)

    def evac_scaled(out, in_, scal, R, Wr):
        state['evac'] += 1
        if state['evac'] % 2 == 0:
            act(out, in_, AF.Identity, R, Wr, scale=scal)
        else:
            ts(out, in_, scal, ALU.mult, R, Wr)

    def rstd_from_ss(stb, n):
        act(stb[:, 1:2], stb[:, 0:1], AF.Ln, [stb], [stb], scale=1.0 / n, bias=EPS)
        act(stb[:, 2:3], stb[:, 1:2], AF.Exp, [stb], [stb], scale=-0.5)

    def w1get(src, kcs=16, bias=None):
        parts = [((lambda b, kcs=kcs: b[:, 0:kcs, :]), src[0])]
        rds = [src[1]]
        if bias is not None:
            parts.append(((lambda b: b[0:1, 16, :]), bias[0]))
            rds.append(bias[1])
        return W1.get(parts, reads=rds)

    def w2get(src):
        return W2.get([((lambda b: b[:]), src[0])], reads=[src[1]])

    KVS = {'next': 0, 'n': 0}

    def pc_kv(upto):
        f2 = w2.f.rearrange("b p (h k) c -> (b p h) (k c)", h=2)
        b2 = w2.b.rearrange("b p (h k) c -> (b p h) (k c)", h=2)
        while KVS['next'] < min(upto, B2_Z):
            blk = KVS['next']
            KVS['next'] += 1
            for hh in range(2):
                a = blk * 256 + hh * 128
                k = KVS['n']
                KVS['n'] += 1
                bnc = W1.bufs[k % 4]
                flat = bnc[:, 0:16, :].rearrange("p k c -> p (k c)")
                S.dma('pool', 'pck%d' % (k % 4), flat, f2[a:a + 128], writes=[bnc])
                S.dma_mark('sp', 'pcs_w2_%d' % blk, b2[a:a + 128], flat, [bnc], w2.bufs[blk])

    def precast(ds, blk0=None, blk1=None):
        shp = ds.f.shape
        if len(shp) == 4:
            f2 = ds.f.rearrange("b p k c -> (b p) (k c)")
            b2 = ds.b.rearrange("b p k c -> (b p) (k c)")
        else:
            f2 = ds.f.rearrange("b p c -> (b p) c")
            b2 = ds.b.rearrange("b p c -> (b p) c")
        rpb = shp[1]
        if ds.perblock:
            for i in range(blk0, blk1):
                S.dma_nw('pool', 'pc_%s_%d' % (ds.buf.name, i), b2[i * rpb:(i + 1) * rpb], f2[i * rpb:(i + 1) * rpb], ds.bufs[i])
            return
        R = f2.shape[0]
        rows_per = 64 if f2.shape[1] >= 2048 else 256
        for a in range(0, R, rows_per):
            e = min(R, a + rows_per)
            S.dma_nw('pool', 'pc_%s' % ds.buf.name, b2[a:e], f2[a:e], ds.buf)

    def dump(i, buf, ap):
        if dbg:
            S.dma('pool', 'dbg', dbgd[i], ap, reads=[buf])

    def load_norm(src, row0, ntiles, gcol, dstbuf, dst_fn, junk):
        prep_tile(src, row0, 0, gcol, dstbuf, dst_fn, junk, 'a')
        for i in range(ntiles):
            if i + 1 < ntiles:
                prep_tile(src, row0, i + 1, gcol, dstbuf, dst_fn, junk, 'a')
            prep_tile(src, row0, i, gcol, dstbuf, dst_fn, junk, 'b')

    def prep_tile(src, row0, i, gcol, dstbuf, dst_fn, junk, part='ab'):
        xb = XB[i % 2]
        if 'a' in part:
            S.dma('sp', 'xb%d' % (i % 2), xb[:], src[row0 + i * 128: row0 + (i + 1) * 128, :], writes=[xb])
            sb_ = stt_buf()
            act(junk[:], xb[:], AF.Square, [xb], [junk, sb_], accum_out=sb_[:, 0:1])
            rstd_from_ss(sb_, D)
            ts(xb[:], xb[:], sb_[:, 2:3], ALU.mult, [xb, sb_], [xb])
        if 'b' in part:
            for q in range(4):
                pb = bank()
                for j in range(4):
                    kc = q * 4 + j
                    tr(pb[:, j * 128:(j + 1) * 128], xb[:, kc * 128:(kc + 1) * 128], ident_f, [xb, CF], [pb])
                for j in range(4):
                    kc = q * 4 + j
                    evac_scaled(dst_fn(kc, i), pb[:, j * 128:(j + 1) * 128], VEC[:, gcol + kc:gcol + kc + 1], [pb, VEC], [dstbuf])

    def alpha_proj(HT=HT):
        pb = bank()
        for kc in range(KC):
            mm(pb[0:16, :], WAL[:, kc, :], HT[:, kc, :], kc == 0, kc == KC - 1, [WAL, HT], [pb])
        cp(ALT[0:16, :], pb[0:16, :], [pb], [ALT], eng='act')

    def decay_tile(i, c0, ncol, E, L, E3, e3_fn):
        decay_half_A(i, c0, ncol, E, L)
        decay_half_B(i, c0, ncol, L, E3, e3_fn)

    def decay_half_A(i, c0, ncol, E, L):
        for c in range(0, ncol, 512):
            w = min(512, ncol - c)
            pb = bank()
            mm(pb[:, 0:w], ALT[0:33, i * 128:(i + 1) * 128], WA2[0:33, c0 + c:c0 + c + w], True, True, [ALT, WA2], [pb])
            act(E[:, c:c + w], pb[:, 0:w], AF.Exp, [pb], [E], scale=-1.0)
        act(L[:, 0:ncol], E[:, 0:ncol], AF.Ln, [E], [L], bias=1.0)

    def decay_half_B(i, c0, ncol, L, E3, e3_fn):
        for c in range(0, ncol, 512):
            w = min(512, ncol - c)
            pb = bank()
            mm(pb[:, 0:w], tri_dec, L[:, c:c + w], True, True, [CB, L], [pb])
            act(e3_fn(c, w), pb[:, 0:w], AF.Exp, [pb], [E3])
        pb = bank()
        nkb = ncol // 128
        for kb in range(nkb):
            mm(pb[:, kb:kb + 1], L[:, kb * 128:(kb + 1) * 128], negcol, True, True, [L, CB], [pb])
        kb0 = c0 // 128
        act(AL[:, i, kb0:kb0 + nkb], pb[:, 0:nkb], AF.Exp, [pb], [AL])

    def state_update(i, h, KDEC, kdec_fn, VT, vt_ap, to_bf):
        for kb2 in range(2):
            kb = 2 * h + kb2
            pb = bank()
            mm(pb[:], kdec_fn(i, kb2), vt_ap, True, True, [KDEC, VT], [pb])
            stt(S32[kb][:], S32[kb][:], AL[:, i, kb:kb + 1], pb[:], ALU.mult, ALU.add, [S32[kb], AL, pb], [S32[kb]])
            if to_bf:
                cp(SBF[kb][:], S32[kb][:], [S32[kb]], [SBF[kb]], eng='act')

    def prefix_bufs():
        arena_reset(0)
        junk = carve([D], BF16, 'junk')
        E = carve([1024], F32, 'E')
        Ls = [carve([1024], BF16, 'L%d' % i) for i in range(2)]
        E3 = carve([NT, 1024], F32, 'E3')
        KDEC = carve([NT, 1024], BF16, 'KDEC')
        VT = carve([NT, D], BF16, 'VT')
        HT2 = carve([KC, G], BF16, 'HT2')
        return junk, E, Ls, E3, KDEC, VT, HT2

    def prefix_group(g, last, pbufs, HT_main=HT):
        junk, E, Ls, E3, KDEC, VT, HT2 = pbufs
        mark('p%d' % g)
        HT = HT_main if g % 2 == 0 else HT2
        HTn = HT2 if g % 2 == 0 else HT_main

        def prep_next(i, part='ab'):
            if not last:
                prep_tile(xp, (g + 1) * G, i, V_ING, HTn, lambda kc, i: HTn[:, kc, i * 128:(i + 1) * 128], junk, part)

        if g == 0:
            load_norm(xp, 0, NT, V_ING, HT, lambda kc, i: HT[:, kc, i * 128:(i + 1) * 128], junk)
        if last:
            cp(HTH[:], HT[:, :, G - 32:G], [HT], [HTH], eng='dve')
        alpha_proj(HT)

        def k_block(nb):
            w = w2get(w2[B2_K + nb])
            for i in range(NT):
                pb = bank()
                for kc in range(KC):
                    mm(pb[:, 0:256], HT[:, kc, i * 128:(i + 1) * 128], w[:, kc, :], kc == 0, kc == KC - 1, [HT, w], [pb])
                tt(KDEC[:, i, nb * 256:(nb + 1) * 256], pb[:, 0:256], E3[:, i, nb * 256:(nb + 1) * 256], ALU.mult, [pb, E3], [KDEC])

        def v_block(nb):
            w = w2get(w2[B2_V + nb])
            for i in range(NT):
                pb = bank()
                for kc in range(KC):
                    mm(pb[:, 0:256], HT[:, kc, i * 128:(i + 1) * 128], w[:, kc, :], kc == 0, kc == KC - 1, [HT, w], [pb])
                cp(VT[:, i, nb * 256:(nb + 1) * 256], pb[:, 0:256], [pb], [VT], eng=('act' if (nb + i) % 2 else 'dve'))

        if g == 0:
            for i in range(NT):
                decay_tile(i, 0, 1024, E, Ls[i % 2], E3, lambda c, w, i=i: E3[:, i, c:c + w])
            for nb in range(4):
                if nb == 1:
                    prep_next(0, 'a')
                if nb == 2:
                    prep_next(0, 'b')
                pc_kv(B2_K + nb + 3)
                k_block(nb)
            for nb in range(8):
                if nb in (0, 3, 6):
                    prep_next({0: 1, 3: 2, 6: 3}[nb], 'a')
                if nb in (1, 4, 7):
                    prep_next({1: 1, 4: 2, 7: 3}[nb], 'b')
                pc_kv(B2_V + nb + 3)
                v_block(nb)
        else:
            for nb in range(8):
                if nb < NT:
                    decay_half_A(nb, 0, 1024, E, Ls[nb % 2])
                if nb in (1, 4, 7):
                    prep_next({1: 0, 4: 1, 7: 2}[nb], 'a')
                if nb in (2, 5):
                    prep_next({2: 0, 5: 1}[nb], 'b')
                v_block(nb)
                if nb < NT:
                    decay_half_B(nb, 0, 1024, Ls[nb % 2], E3, lambda c, w, i=nb: E3[:, i, c:c + w])
            for nb in range(4):
                if nb == 1:
                    prep_next(3, 'a')
                if nb in (0, 2):
                    prep_next({0: 2, 2: 3}[nb], 'b')
                k_block(nb)
        if g == 0:
            pc_kv(B2_Z)
            background_precast()
        for i in range(NT):
            for h in range(4):
                state_update(i, h, KDEC, lambda i, kb2, h=h: KDEC[:, i, (2 * h + kb2) * 128:(2 * h + kb2 + 1) * 128],
                             VT, VT[:, i, h * 512:(h + 1) * 512], False)

    def startup():
        S.dma('sp', 'c0', CF[:], cf[:, :], writes=[CF])
        S.dma('sp', 'c0', VEC[:], vecd[:, :], writes=[VEC])
        S.dma('sp', 'c0', HW[:], dwwd[:, :, :], writes=[HW])
        S.dma('sp', 'c0', FG[:], fgd.partition_broadcast(128), writes=[FG])
        S.dma('pool', 'c1', CB[:], cbd[:, :], writes=[CB])
        S.dma('pool', 'c1', WA2[:], wa2d[:, :], writes=[WA2])
        S.dma('pool', 'c1', WAL[:], wal[:, :, :], writes=[WAL])
        precast(bcod)

    def background_precast():
        precast(wmk)
        precast(wmv)
        precast(w1)
        precast(w2, B2_Z, NB2)
        for ds in (wco, wgo, wmo, wout):
            precast(ds)
        S.op('dve', 'memset', dict(ap=ONES1[:], constant=1.0), [], [ONES1])
        S.op('dve', 'memset', dict(ap=ALT[:], constant=0.0), [], [ALT])
        S.op('dve', 'memset', dict(ap=ALT[32:33, :], constant=1.0), [], [ALT])
        for kb in range(8):
            S.op('dve', 'memset', dict(ap=S32[kb][:], constant=0.0), [], [S32[kb]])
        ts(HB[:], VEC[:, V_BG:V_BG + 48], 0.5, ALU.mult, [VEC], [HB])
        ts(HW[:], HW[:], 0.5, ALU.mult, [HW], [HW])

    def dg_build():
        for cbk in range(16):
            for half in range(2):
                b = DG.bufs[(cbk * 2 + half) % 2]
                ntap = 16 if half == 0 else 15
                for jj in range(ntap):
                    j = half * 16 + jj
                    ts(b[:, jj, :], ident_f, HW[:, cbk, j:j + 1], ALU.mult, [CF, HW], [b])
                if half == 1:
                    S.op('dve', 'memset', dict(ap=b[:, 15, :], constant=0.0), [], [b])
                S.dma('sp', 'dgw', dgs[cbk, half], b[:], reads=[b], writes=[DGD])

    def mem_kv():
        arena_reset(0)
        junk = carve([D], BF16, 'junk')
        MT = carve([KC, 256], BF16, 'MT')
        load_norm(memd, 0, 2, V_MLG, MT, lambda kc, i: MT[:, kc, i * 128:(i + 1) * 128], junk)
        for h in range(4):
            w = w1get(wmk[h])
            pb = bank()
            for kc in range(KC):
                mm(pb[:, 0:256], w[:, kc, :], MT[:, kc, :], kc == 0, kc == KC - 1, [w, MT], [pb])
            cp(MKT[:, h, :], pb[:, 0:256], [pb], [MKT], eng='act')
        for nb in range(2):
            w = w2get(wmv[nb])
            for mb in range(2):
                pb = bank()
                for kc in range(KC):
                    mm(pb[:, 0:256], MT[:, kc, mb * 128:(mb + 1) * 128], w[:, kc, :], kc == 0, kc == KC - 1, [MT, w], [pb])
                cp(MV[:, mb, nb * 256:(nb + 1) * 256], pb[:, 0:256], [pb], [MV], eng='dve')
        S.barrier()

    def halo_stage():
        arena_reset(0)
        THB = carve([32], F32, 'thb')
        for cbk in range(16):
            wa = w1get(w1[B1_A + cbk])
            pa = bank()
            for kc in range(KC):
                mm(pa[:, 0:32], wa[:, kc, :], HTH[:, kc, :], kc == 0, kc == KC - 1, [wa, HTH], [pa])
            wb = w1get(w1[B1_B + cbk])
            pb = bank()
            for kc in range(KC):
                mm(pb[:, 0:32], wb[:, kc, :], HTH[:, kc, :], kc == 0, kc == KC - 1, [wb, HTH], [pb])
            act(THB[:], pb[:, 0:32], AF.Tanh, [pb], [THB], scale=0.5)
            stt(UH[:, cbk, :], THB[:], 1.0, pa[:, 0:32], ALU.add, ALU.mult, [THB, pa], [UH])
        for kb in range(8):
            cp(SBF[kb][:], S32[kb][:], [S32[kb]], [SBF[kb]], eng='act')
        S.barrier()

    def main_group(g):
        arena_reset(0)
        UF = carve([16, G], BF16, 'UF')
        OZ = carve([16, G], BF16, 'OZ')
        OMZ = carve([4, G], BF16, 'OMZ')
        base = state['apos']

        mark('g%d_s0' % g)
        junk = carve([D], BF16, 'junk')
        load_norm(xm, g * G, NT, V_ING, HT, lambda kc, i: HT[:, kc, i * 128:(i + 1) * 128], junk)
        dump(0, HT, HT[:, 0, :])
        S.barrier()

        mark('g%d_s1' % g)
        arena_reset(base)
        YC = carve([16, G], BF16, 'YC')
        Us = [carve([32 + G], BF16, 'U%d' % i) for i in range(2)]
        YSQs = [carve([G], BF16, 'YSQ%d' % i) for i in range(2)]
        THBs = [carve([G], F32, 'THB%d' % i) for i in range(2)]
        T1s = [carve([G], F32, 'T1_%d' % i) for i in range(2)]
        T2s = [carve([G], F32, 'T2_%d' % i) for i in range(2)]
        MU = carve([G], F32, 'MU')
        MSQ = carve([G], F32, 'MSQ')
        P6, P7 = PSB[6], PSB[7]
        for cbk in range(16):
            wa = w1get(w1[B1_A + cbk])
            pa = bank()
            for kc in range(KC):
                mm(pa[:], wa[:, kc, :], HT[:, kc, :], kc == 0, kc == KC - 1, [wa, HT], [pa])
            wb = w1get(w1[B1_B + cbk])
            pb = bank()
            for kc in range(KC):
                mm(pb[:], wb[:, kc, :], HT[:, kc, :], kc == 0, kc == KC - 1, [wb, HT], [pb])
            wz = w1get(w1[B1_Z + cbk])
            pz = bank()
            for kc in range(KC):
                mm(pz[:], wz[:, kc, :], HT[:, kc, :], kc == 0, kc == KC - 1, [wz, HT], [pz])
            thb = THBs[cbk % 2]
            U = Us[cbk % 2]
            act(thb[:], pb[:], AF.Tanh, [pb], [thb], scale=0.5)
            cp(U[:, 0:32], UH[:, cbk, :], [UH], [U], eng='dve')
            stt(U[:, 32:32 + G], thb[:], 1.0, pa[:], ALU.add, ALU.mult, [thb, pa], [U])
            cp(UH[:, cbk, :], U[:, G:G + 32], [U], [UH], eng='dve')
            act(UF[:, cbk, :], pz[:], AF.Silu, [pz], [UF])
            pc = bank()
            dgx = DG.get([((lambda b: b[:]), dgs[cbk, 0])], reads=[DGD])
            for j in range(16):
                mm(pc[:], dgx[:, j, :], U[:, 2 + j:2 + j + G], j == 0, False, [dgx, U], [pc])
            dgx = DG.get([((lambda b: b[:]), dgs[cbk, 1])], reads=[DGD])
            for j in range(16, 31):
                mm(pc[:], dgx[:, j - 16, :], U[:, 2 + j:2 + j + G], False, j == 30, [dgx, U], [pc])
            ysq = YSQs[cbk % 2]
            act(YC[:, cbk, :], pc[:], AF.Identity, [pc, VEC], [YC], bias=VEC[:, V_DWB + cbk:V_DWB + cbk + 1])
            act(ysq[:], pc[:], AF.Square, [pc, VEC], [ysq], bias=VEC[:, V_DWB + cbk:V_DWB + cbk + 1])
            mm(P6[:], ones_b, YC[:, cbk, :], cbk == 0, cbk == 15, [CB, YC], [P6], sig=True)
            mm(P7[:], ones_b, ysq[:], cbk == 0, cbk == 15, [CB, ysq], [P7], sig=True)
        mark('g%d_s1ln' % g)
        act(MU[:], P6[:], AF.Copy, [P6], [MU], scale=1.0 / D)
        tt(MSQ[:], MU[:], MU[:], ALU.mult, [MU], [MSQ])
        stt(MSQ[:], P7[:], 1.0 / D, MSQ[:], ALU.mult, ALU.subtract, [P7, MSQ], [MSQ])
        ts(MSQ[:], MSQ[:], 0.0, ALU.max, [MSQ], [MSQ])
        act(MSQ[:], MSQ[:], AF.Ln, [MSQ], [MSQ], bias=EPS)
        act(P6[:], MSQ[:], AF.Exp, [MSQ], [P6], scale=-0.5)
        stt(P7[:], MU[:], -1.0, P6[:], ALU.mult, ALU.mult, [MU, P6], [P7])
        for cbk in range(16):
            t1 = T1s[cbk % 2]
            t2 = T2s[cbk % 2]
            tt(t1[:], YC[:, cbk, :], P6[:], ALU.mult, [YC, P6], [t1])
            tt(t1[:], t1[:], P7[:], ALU.add, [t1, P7], [t1])
            act(t2[:], t1[:], AF.Silu, [t1, VEC], [t2], scale=VEC[:, V_LNG + cbk:V_LNG + cbk + 1], bias=VEC[:, V_LNB + cbk:V_LNB + cbk + 1])
            tt(UF[:, cbk, :], t2[:], UF[:, cbk, :], ALU.mult, [t2, UF], [UF], eng='pool')
        dump(1, UF, UF[:, 0, :])
        dump(5, YC, YC[:, 0, :])
        S.barrier()

        mark('g%d_s3' % g)
        arena_reset(base)
        junk = carve([512], BF16, 'junk')
        E = carve([256], F32, 'E')
        Lh = carve([NT, 256], BF16, 'Lh')
        E3 = carve([NT, 256], F32, 'E3')
        E1s = [carve([G], F32, 'E1_%d' % i) for i in range(2)]
        E2s = [carve([G], F32, 'E2_%d' % i) for i in range(2)]
        QE = carve([2, G], BF16, 'QE')
        KD = carve([2, G], BF16, 'KD')
        KDEC = carve([NT, 256], BF16, 'KDEC')
        VT = carve([NT, 512], BF16, 'VT')
        ZT = carve([NT, 512], BF16, 'ZT')
        PTs = [carve([128], BF16, 'PT%d' % i) for i in range(2)]
        OTs = [carve([512], F32, 'OT%d' % i) for i in range(2)]
        alpha_proj()
        for h in range(4):
            for i in range(NT):
                _decay_tile_h(i, h, E, Lh, E3)
            for kb2 in range(2):
                pbf = bank()
                for i in range(NT):
                    mm(pbf[:, i * 128:(i + 1) * 128], Lh[:, i, kb2 * 128:(kb2 + 1) * 128], tri_inc, True, True, [Lh, CB], [pbf])
                e1 = E1s[kb2]
                e2 = E2s[kb2]
                act(e1[:], pbf[:], AF.Exp, [pbf, LN16], [e1], bias=LN16[:, 0:1])
                act(e2[:], pbf[:], AF.Exp, [pbf], [e2], scale=-1.0)
                wq = w1get(w1[B1_Q + 2 * h + kb2])
                pq = bank()
                for kc in range(KC):
                    mm(pq[:], wq[:, kc, :], HT[:, kc, :], kc == 0, kc == KC - 1, [wq, HT], [pq])
                tt(QE[:, kb2, :], pq[:], e1[:], ALU.mult, [pq, e1], [QE])
                wk = w1get(w1[B1_K + 2 * h + kb2])
                pk = bank()
                for kc in range(KC):
                    mm(pk[:], wk[:, kc, :], HT[:, kc, :], kc == 0, kc == KC - 1, [wk, HT], [pk])
                tt(KD[:, kb2, :], pk[:], e2[:], ALU.mult, [pk, e2], [KD])
            w = w2get(w2[B2_K + h])
            for i in range(NT):
                pb = bank()
                for kc in range(KC):
                    mm(pb[:, 0:256], HT[:, kc, i * 128:(i + 1) * 128], w[:, kc, :], kc == 0, kc == KC - 1, [HT, w], [pb])
                tt(KDEC[:, i, :], pb[:, 0:256], E3[:, i, :], ALU.mult, [pb, E3], [KDEC])
            for nb2 in range(2):
                w = w2get(w2[B2_V + 2 * h + nb2])
                for i in range(NT):
                    pb = bank()
                    for kc in range(KC):
                        mm(pb[:, 0:256], HT[:, kc, i * 128:(i + 1) * 128], w[:, kc, :], kc == 0, kc == KC - 1, [HT, w], [pb])
                    cp(VT[:, i, nb2 * 256:(nb2 + 1) * 256], pb[:, 0:256], [pb], [VT], eng=('act' if i % 2 else 'dve'))
            for nb2 in range(2):
                w = w2get(w2[B2_Z + 2 * h + nb2])
                for i in range(NT):
                    pb = bank()
                    for kc in range(KC):
                        mm(pb[:, 0:256], HT[:, kc, i * 128:(i + 1) * 128], w[:, kc, :], kc == 0, kc == KC - 1, [HT, w], [pb])
                    act(ZT[:, i, nb2 * 256:(nb2 + 1) * 256], pb[:, 0:256], AF.Silu, [pb], [ZT])
            for i in range(NT):
                tsl = slice(i * 128, (i + 1) * 128)
                psc = bank()
                for kb2 in range(2):
                    mm(psc[:, 0:128], KD[:, kb2, tsl], QE[:, kb2, tsl], kb2 == 0, kb2 == 1, [KD, QE], [psc])
                pt = PTs[i % 2]
                tt(pt[:], psc[:, 0:128], mask_f, ALU.mult, [psc, CF], [pt])
                po = bank()
                mm(po[:], QE[:, 0, tsl], SBF[2 * h][:], True, False, [QE, SBF[2 * h]], [po])
                mm(po[:], QE[:, 1, tsl], SBF[2 * h + 1][:], False, False, [QE, SBF[2 * h + 1]], [po])
                mm(po[:], pt[:], VT[:, i, :], False, True, [pt, VT], [po])
                state_update(i, h, KDEC, lambda i, kb2: KDEC[:, i, kb2 * 128:(kb2 + 1) * 128], VT, VT[:, i, :], True)
                sb_ = stt_buf()
                act(junk[:], po[:], AF.Square, [po], [junk, sb_], accum_out=sb_[:, 0:1])
                rstd_from_ss(sb_, 512)
                ot = OTs[i % 2]
                stt(ot[:], po[:], sb_[:, 2:3], ZT[:, i, :], ALU.mult, ALU.mult, [po, sb_, ZT], [ot])
                ptb = bank()
                for vb in range(4):
                    tr(ptb[:, vb * 128:(vb + 1) * 128], ot[:, vb * 128:(vb + 1) * 128], ident_f, [ot, CF], [ptb])
                for vb in range(4):
                    c = 4 * h + vb
                    evac_scaled(OZ[:, c, tsl], ptb[:, vb * 128:(vb + 1) * 128], VEC[:, V_GNG + c:V_GNG + c + 1], [ptb, VEC], [OZ])
        dump(2, OZ, OZ[:, 0, :])
        dump(6, OZ, OZ[:, 4, :])
        S.barrier()

        mark('g%d_s4' % g)
        arena_reset(base)
        MQ = carve([4, G], BF16, 'MQ')
        SMZ = carve([4, G], BF16, 'SMZ')
        Ps = [carve([256], F32, 'P%d' % i) for i in range(2)]
        PNs = [carve([256], BF16, 'PN%d' % i) for i in range(2)]
        PTm = [carve([256], BF16, 'PTm%d' % i) for i in range(2)]
        for h in range(4):
            w = w1get(w1[B1_MQ + h])
            pb = bank()
            for kc in range(KC):
                mm(pb[:], w[:, kc, :], HT[:, kc, :], kc == 0, kc == KC - 1, [w, HT], [pb])
            act(MQ[:, h, :], pb[:], AF.Copy, [pb], [MQ], scale=float(128 ** -0.5))
            w = w1get(w1[B1_MZ + h])
            pb = bank()
            for kc in range(KC):
                mm(pb[:], w[:, kc, :], HT[:, kc, :], kc == 0, kc == KC - 1, [w, HT], [pb])
            act(SMZ[:, h, :], pb[:], AF.Silu, [pb], [SMZ])
        its = [(i, h) for i in range(NT) for h in range(4)]

        def st_A(n):
            i, h = its[n]
            tsl = slice(i * 128, (i + 1) * 128)
            psc = bank()
            mm(psc[:, 0:256], MQ[:, h, tsl], MKT[:, h, :], True, True, [MQ, MKT], [psc])
            sb_ = stt_buf()
            S.op('dve', 'tensor_reduce', dict(out=sb_[:, 0:1], in_=psc[:, 0:256], axis=AX.X, op=ALU.max), [psc], [sb_])
            ts(sb_[:, 1:2], sb_[:, 0:1], -1.0, ALU.mult, [sb_], [sb_])
            P = Ps[n % 2]
            PN = PNs[n % 2]
            act(P[:], psc[:, 0:256], AF.Exp, [psc, sb_], [P, sb_], bias=sb_[:, 1:2], accum_out=sb_[:, 2:3])
            S.op('dve', 'reciprocal', dict(out=sb_[:, 3:4], in_=sb_[:, 2:3]), [sb_], [sb_])
            ts(PN[:], P[:], sb_[:, 3:4], ALU.mult, [P, sb_], [PN])

        def st_B(n):
            PN = PNs[n % 2]
            PT = PTm[n % 2]
            ptb = bank()
            ptb16 = ptb[:].bitcast(BF16)
            for mb in range(2):
                tr(ptb16[:, mb * 128:(mb + 1) * 128], PN[:, mb * 128:(mb + 1) * 128], ident_b, [PN, CB], [ptb])
            cp(PT[:], ptb16[:, 0:256], [ptb], [PT], eng='act')

        def st_C(n):
            i, h = its[n]
            tsl = slice(i * 128, (i + 1) * 128)
            PT = PTm[n % 2]
            pom = bank()
            for mb in range(2):
                mm(pom[:, 0:128], MV[:, mb, h * 128:(h + 1) * 128], PT[:, mb * 128:(mb + 1) * 128], mb == 0, mb == 1, [MV, PT], [pom])
            tt(OMZ[:, h, tsl], pom[:, 0:128], SMZ[:, h, tsl], ALU.mult, [pom, SMZ], [OMZ])

        NI = len(its)
        for n in range(NI + 2):
            if n < NI:
                st_A(n)
            if n >= 2:
                st_C(n - 2)
            if 1 <= n <= NI:
                st_B(n - 1)
        dump(3, OMZ, OMZ[:, 0, :])
        S.barrier()

        mark('g%d_s5' % g)
        arena_reset(base)
        YS = carve([16, G], BF16, 'YS')
        THs = [carve([G], F32, 'TH%d' % i) for i in range(2)]
        THs.append(THs[0])
        Ms = [carve([G], F32, 'M%d' % i) for i in range(2)]
        Ms.append(Ms[1])
        for db in range(16):
            for br in range(3):
                if br == 0:
                    wy = w1get(wco[db], bias=bcod[db])
                    src, nk = UF, 16
                elif br == 1:
                    wy = w1get(wgo[db])
                    src, nk = OZ, 16
                else:
                    wy = w1get(wmo[db], kcs=4)
                    src, nk = OMZ, 4
                py = bank()
                for kc in range(nk):
                    mm(py[:], wy[:, kc, :], src[:, kc, :], kc == 0, (kc == nk - 1) and br != 0, [wy, src], [py])
                if br == 0:
                    mm(py[:], wy[0:1, 16, :], ONES1[0:1, :], False, True, [wy, ONES1], [py])
                wg = w1get(w1[B1_G + br * 16 + db])
                pg = bank()
                for kc in range(KC):
                    mm(pg[:], wg[:, kc, :], HT[:, kc, :], kc == 0, kc == KC - 1, [wg, HT], [pg])
                th = THs[br]
                act(th[:], pg[:], AF.Tanh, [pg, HB], [th], scale=0.5, bias=HB[:, br * 16 + db:br * 16 + db + 1])
                stt(Ms[br][:], th[:], 1.0, py[:], ALU.add, ALU.mult, [th, py], [Ms[br]])
                if br == 1:
                    tt(Ms[0][:], Ms[0][:], Ms[1][:], ALU.add, [Ms[0], Ms[1]], [Ms[0]])
            tt(YS[:, db, :], Ms[0][:], Ms[2][:], ALU.add, [Ms[0], Ms[2]], [YS])
        dump(4, YS, YS[:, 0, :])
        mark('g%d_s5out' % g)
        XBs = [XB[0], XB[1], carve([D], F32, 'XB2'), carve([D], F32, 'XB3')]
        for i in range(NT):
            S.dma('sp', 'xo%d' % i, XBs[i][:], xm[g * G + i * 128:g * G + (i + 1) * 128, :], writes=[XBs[i]])
        for nb in range(8):
            w = w2get(wout[nb])
            for i in range(NT):
                xb = XBs[i]
                pb = bank()
                for db in range(16):
                    mm(pb[:, 0:256], YS[:, db, i * 128:(i + 1) * 128], w[:, db, :], db == 0, db == 15, [YS, w], [pb])
                stt(xb[:, nb * 256:(nb + 1) * 256], pb[:, 0:256], 0.5, xb[:, nb * 256:(nb + 1) * 256], ALU.mult, ALU.add, [pb, xb], [xb])
        junkv = YS[:, 0:4, :]
        for i in range(NT):
            xb = XBs[i]
            sb_ = stt_buf()
            act(junkv, xb[:].rearrange("p (a b) -> p a b", a=4), AF.Square, [xb], [YS, sb_], accum_out=sb_[:, 0:1])
            rstd_from_ss(sb_, D)
            stt(xb[:], xb[:], sb_[:, 2:3], FG[:], ALU.mult, ALU.mult, [xb, sb_, FG], [xb])
            S.dma('sp', 'out', outd[g * G + i * 128:g * G + (i + 1) * 128, :], xb[:], reads=[xb])
        S.barrier()

    def _decay_tile_h(i, h, E, Lh, E3):
        c0 = h * 256
        pb = bank()
        mm(pb[:, 0:256], ALT[0:33, i * 128:(i + 1) * 128], WA2[0:33, c0:c0 + 256], True, True, [ALT, WA2], [pb])
        act(E[:], pb[:, 0:256], AF.Exp, [pb], [E], scale=-1.0)
        act(Lh[:, i, :], E[:], AF.Ln, [E], [Lh], bias=1.0)
        pb = bank()
        mm(pb[:, 0:256], tri_dec, Lh[:, i, :], True, True, [CB, Lh], [pb])
        act(E3[:, i, :], pb[:, 0:256], AF.Exp, [pb], [E3])
        pb = bank()
        for kb2 in range(2):
            mm(pb[:, kb2:kb2 + 1], Lh[:, i, kb2 * 128:(kb2 + 1) * 128], negcol, True, True, [Lh, CB], [pb])
        act(AL[:, i, 2 * h:2 * h + 2], pb[:, 0:2], AF.Exp, [pb], [AL])

    LN16 = S.sb([128, 1], F32, 'LN16')

    def mark(name):
        if not S.dry:
            MARKS.append((name, sum(1 for it in S.prog['pe'] if it[0] == 'inst')))

    def emit_all():
        S.op('dve', 'memset', dict(ap=LN16[:], constant=float(np.log(1.0 / 16.0))), [], [LN16])
        startup()
        pbufs = prefix_bufs()
        KVS['next'] = 0
        KVS['n'] = 0
        pc_kv(2)
        for g in range(NPG):
            prefix_group(g, g == NPG - 1, pbufs)
            if g == 0:
                dg_build()
        if NPG == 0:
            dg_build()
        S.barrier()
        mark('memkv')
        mem_kv()
        mark('halo')
        halo_stage()
        for g in range(NG):
            main_group(g)

    S.dry = True
    emit_all()
    S.dry = False
    S.reset()
    state.update({'bank': 0, 'st': 0, 'apos': 0, 'evac': 0})
    for p in (W1, W2, DG):
        p.start_real()
    allb = [CF, CB, VEC, HB, HW, FG, WA2, ONES1, WAL, HT, HTH, UH, MKT, MV, ALT, AL, DGD, LN16] + XB + S32 + SBF + STT + PSB
    for b in allb:
        b.lw = None
        b.rd = {}
    emit_all()
    mark('end')
    S.final_wait('sp', 'out')
    if dbg:
        S.final_wait('pool', 'dbg')
    S.emit()
    st.close()
    return nc


_CACHE = {}
MARKS = []


def _host_consts():
    ident = np.eye(128, dtype=np.float32)
    s = np.arange(128)[:, None]
    t = np.arange(128)[None, :]
    mask = (s <= t).astype(np.float32)
    tri_inc = np.where(s <= t, -1.0 / 16.0, 0.0).astype(np.float32)
    tri_dec = np.where(s > t, -1.0 / 16.0, 0.0).astype(np.float32)
    ones = np.ones((128, 128), np.float32)
    cf = np.concatenate([ident, mask], axis=1)
    cb = np.concatenate([ident, tri_inc, tri_dec, ones], axis=1)
    return np.ascontiguousarray(cf), np.ascontiguousarray(cb)


def _blk1(w, ncols_blk=128):
    K, N = w.shape
    return np.ascontiguousarray(w.reshape(K // 128, 128, N // ncols_blk, ncols_blk).transpose(2, 1, 0, 3))


def _pvec(v):
    return np.ascontiguousarray(v.reshape(-1, 128).T)


def prep_weights(inp):
    w_in = np.asarray(inp["w_in"])[0]
    cols1 = np.concatenate([w_in[:, C_A:C_A + 2048], w_in[:, C_B:C_B + 2048], w_in[:, C_Z:C_Z + 2048],
                            w_in[:, C_Q:C_Q + 1024], w_in[:, C_K:C_K + 1024], w_in[:, C_MQ:C_MQ + 512],
                            w_in[:, C_MZ:C_MZ + 512], w_in[:, C_GATE:C_GATE + 6144]], axis=1)
    cols2 = np.concatenate([w_in[:, C_K:C_K + 1024], w_in[:, C_V:C_V + 2048], w_in[:, C_GZ:C_GZ + 2048]], axis=1)
    cf, cb = _host_consts()
    vec = np.concatenate([_pvec(np.asarray(inp["b_gate"])[0]), _pvec(np.asarray(inp["dw_b"])[0]),
                          _pvec(np.asarray(inp["conv_ln_g"])[0]), _pvec(np.asarray(inp["conv_ln_b"])[0]),
                          _pvec(np.asarray(inp["ln_in_g"])[0]), _pvec(np.asarray(inp["gla_norm_g"])[0]),
                          _pvec(np.asarray(inp["mem_ln_g"])[0])], axis=1)
    dw = np.asarray(inp["dw_w"])[0]
    dww = np.ascontiguousarray(dw.reshape(31, 16, 128).transpose(2, 1, 0))
    wa2 = np.zeros((33, 1024), np.float32)
    wa2[0:16] = np.asarray(inp["w_alpha2"])[0]
    wa2[32] = np.asarray(inp["b_alpha"])[0]
    wkv = np.asarray(inp["w_mem_kv"])[0]
    d = {
        "w1": _blk1(cols1), "w2": _blk1(cols2, 256),
        "wal": np.ascontiguousarray(w_in[:, C_AL:C_AL + 16].reshape(16, 128, 16).transpose(1, 0, 2)),
        "wco": _blk1(np.asarray(inp["w_conv_out"])[0]), "wgo": _blk1(np.asarray(inp["w_gla_out"])[0]),
        "wmo": _blk1(np.asarray(inp["w_mem_out"])[0]), "wout": _blk1(np.asarray(inp["w_out"])[0], 256),
        "wmk": _blk1(wkv[:, 0:512]), "wmv": _blk1(wkv[:, 512:1024], 256),
        "cf": cf, "cb": cb, "vec": np.ascontiguousarray(vec.astype(np.float32)), "dww": dww.astype(np.float32),
        "wa2": wa2, "bco": np.ascontiguousarray(np.asarray(inp["b_conv_out"])[0].reshape(16, 1, 128)),
        "fg": np.ascontiguousarray(np.asarray(inp["final_g"]).astype(np.float32)),
    }
    return d


def run(inp, NG=NG_FULL, NPG=NPG_FULL, cores=N_CORES, dbg=False):
    key = (NG, NPG, dbg)
    nc = bass.Bass("TRN2", target_bir_lowering=False)
    build(nc, NG, NPG, dbg)
    wd = prep_weights(inp)
    x = np.asarray(inp["x"])
    mem = np.asarray(inp["mem"])
    SEG = x.shape[1] // 4
    in_maps = []
    for c in range(cores):
        b, j = c // 4, c % 4
        s0 = j * SEG
        xm = np.ascontiguousarray(x[b, s0:s0 + NG * G])
        npre = NPG * G
        xp = np.zeros((npre, D), np.float32)
        take = min(npre, s0)
        if take > 0:
            xp[npre - take:] = x[b, s0 - take:s0]
        m = dict(wd)
        m["xm"] = xm
        m["xp"] = xp
        m["mem"] = np.ascontiguousarray(mem[b])
        in_maps.append(m)
    res = run_bass_kernel_spmd(nc, in_maps, core_ids=list(range(cores)))
    return res


def kernel(**inputs):
    res = run(inputs)
    x = np.asarray(inputs["x"])
    out = np.empty(x.shape, np.float32)
    SEG = x.shape[1] // 4
    for c in range(N_CORES):
        b, j = c // 4, c % 4
        out[b, j * SEG:(j + 1) * SEG] = res.results[c]["out"]
    return out
```

```python
import contextlib
import numpy as np
import concourse.bass as bass
import concourse.mybir as mybir
from concourse.bass_utils import run_bass_kernel_spmd

F32 = mybir.dt.float32
BF16 = mybir.dt.bfloat16
U8 = mybir.dt.uint8
AF = mybir.ActivationFunctionType
ALU = mybir.AluOpType
AX = mybir.AxisListType

D = 2048
KC = 16
G = 512
NT = 4
EPS = 1e-6
N_CORES = 8
NG_FULL = 4
NPG_FULL = 12

C_A, C_B, C_Z, C_Q, C_K, C_V, C_GZ, C_AL, C_MQ, C_MZ, C_GATE = 0, 2048, 4096, 6144, 7168, 8192, 10240, 12288, 12304, 12816, 13328
B1_A, B1_B, B1_Z, B1_Q, B1_K, B1_MQ, B1_MZ, B1_G = 0, 16, 32, 48, 56, 64, 68, 72
NB1 = 120
B2_K, B2_V, B2_Z = 0, 4, 12
NB2 = 20
V_BG, V_DWB, V_LNG, V_LNB, V_ING, V_GNG, V_MLG, NV = 0, 48, 64, 80, 96, 112, 128, 144


class Buf:
    def __init__(self, t, name=''):
        self.t = t
        self.name = name
        self.lw = None
        self.rd = {}

    def __getitem__(self, idx):
        return self.t[idx]


class Sched:
    ENG = ['pe', 'act', 'dve', 'pool', 'sp']

    def __init__(self, nc, st, same_sync=('act', 'dve', 'pool')):
        self.nc = nc
        self.st = st
        self.sem = {n: st.enter_context(nc.semaphore('s_' + n)) for n in self.ENG}
        self.dsem_h = {}
        self.same_sync = set(same_sync)
        self.nsb = 0
        self.dry = False
        self.reset()

    def reset(self):
        self.prog = {n: [] for n in self.ENG}
        self.cnt = {n: 0 for n in self.ENG}
        self.waited = {n: {} for n in self.ENG}
        self.dcnt = {}
        self.pend = {n: False for n in self.ENG}
        self.bufs = []

    def track(self, b):
        self.bufs.append(b)
        return b

    def sb(self, shape, dt, name=None):
        self.nsb += 1
        name = name or ('sb%d' % self.nsb)
        return Buf(self.st.enter_context(self.nc.sbuf_tensor(name, list(shape), dt)), name)

    def ps(self, shape, dt, name=None):
        self.nsb += 1
        name = name or ('ps%d' % self.nsb)
        return Buf(self.st.enter_context(self.nc.psum_tensor(name, list(shape), dt)), name)

    def dma_sem(self, name):
        if name not in self.dsem_h:
            self.dsem_h[name] = self.st.enter_context(self.nc.semaphore('d_' + name))
        if name not in self.dcnt:
            self.dcnt[name] = 0
        return self.dsem_h[name]

    def _wait(self, eng, evs):
        w = self.waited[eng]
        need = {}
        for ev in evs:
            if ev is None:
                continue
            s, v, src = ev
            if src == eng and eng not in self.same_sync:
                continue
            k = id(s)
            if w.get(k, 0) >= v:
                continue
            if k not in need or need[k][1] < v:
                need[k] = (s, v)
        for k, (s, v) in need.items():
            self.prog[eng].append(('wait', s, v))
            w[k] = v

    def _deps(self, reads, writes):
        evs = []
        for b in reads:
            evs.append(b.lw)
        for b in writes:
            evs.append(b.lw)
            evs.extend(b.rd.values())
        return evs

    def _mark(self, ev, reads, writes):
        k = id(ev[0])
        for b in reads:
            if k not in b.rd or b.rd[k][1] < ev[1]:
                b.rd[k] = ev
        for b in writes:
            b.lw = ev
            b.rd = {}

    def op(self, eng, meth, kw, reads=(), writes=(), signal=True):
        if self.dry:
            return
        self._wait(eng, self._deps(reads, writes))
        ev = (self.sem[eng], self.cnt[eng] + 1, eng)
        if signal:
            self.cnt[eng] += 1
            self.pend[eng] = False
        else:
            self.pend[eng] = True
        self.prog[eng].append(('inst', meth, kw, self.sem[eng] if signal else None, 1))
        self._mark(ev, reads, writes)

    def dma(self, eng, semname, out, in_, reads=(), writes=()):
        if self.dry:
            return
        self._wait(eng, self._deps(reads, writes))
        s = self.dma_sem(semname)
        self.dcnt[semname] += 16
        ev = (s, self.dcnt[semname], 'dma')
        self.prog[eng].append(('inst', 'dma_start', dict(out=out, in_=in_), s, 16))
        self._mark(ev, reads, writes)

    def dma_mark(self, eng, semname, out, in_, reads, markbuf):
        if self.dry:
            return
        self._wait(eng, self._deps(reads, []))
        s = self.dma_sem(semname)
        self.dcnt[semname] += 16
        ev = (s, self.dcnt[semname], 'dma')
        self.prog[eng].append(('inst', 'dma_start', dict(out=out, in_=in_), s, 16))
        self._mark(ev, reads, [])
        markbuf.lw = ev
        markbuf.rd = {}

    def dma_nw(self, eng, semname, out, in_, buf):
        if self.dry:
            return
        s = self.dma_sem(semname)
        self.dcnt[semname] += 16
        self.prog[eng].append(('inst', 'dma_start', dict(out=out, in_=in_), s, 16))
        buf.lw = (s, self.dcnt[semname], 'dma')
        buf.rd = {}

    def barrier(self):
        if self.dry:
            return
        for e in self.ENG:
            assert not self.pend[e]
        evs = [(self.sem[f], self.cnt[f], f) for f in self.ENG]
        evs += [(self.dsem_h[n], c, 'dma') for n, c in self.dcnt.items()]
        for e in self.ENG:
            self._wait(e, [ev for ev in evs if ev[2] != e and ev[1] > 0])

    def final_wait(self, eng, semname):
        self.prog[eng].append(('wait', self.dsem_h[semname], self.dcnt[semname]))

    def emit(self):
        nc = self.nc
        block = self.st.enter_context(nc.Block())

        def replay(name, eng):
            for it in self.prog[name]:
                if it[0] == 'wait':
                    eng.wait_ge(it[1], it[2])
                else:
                    ins = getattr(eng, it[1])(**it[2])
                    if it[3] is not None:
                        ins.then_inc(it[3], it[4])

        @block.tensor
        def _(e):
            replay('pe', nc.tensor)

        @block.scalar
        def _(e):
            replay('act', nc.scalar)

        @block.vector
        def _(e):
            replay('dve', nc.vector)

        @block.gpsimd
        def _(e):
            replay('pool', nc.gpsimd)

        @block.sync
        def _(e):
            replay('sp', nc.sync)


class WPool:
    def __init__(self, S, n, shape, dt, name, eng):
        self.S = S
        self.n = n
        self.name = name
        self.eng = eng
        self.bufs = [S.sb(shape, dt, '%s_%d' % (name, i)) for i in range(n)]
        self.seq = []
        self.pos = 0
        self.issued = 0

    def start_real(self):
        self.pos = 0
        self.issued = 0
        for b in self.bufs:
            b.lw = None
            b.rd = {}

    def get(self, parts, reads=()):
        S = self.S
        if S.dry:
            self.seq.append((parts, reads))
            self.pos += 1
            return self.bufs[(self.pos - 1) % self.n]
        idx = self.pos
        lim = min(len(self.seq), idx + self.n)
        while self.issued < lim:
            j = self.issued
            b = self.bufs[j % self.n]
            p, r = self.seq[j]
            if any(rb.lw is None for rb in r):
                assert j > idx, 'weight source not ready'
                break
            for (dstf, src) in p:
                S.dma(self.eng, '%s_%d' % (self.name, j % self.n), dstf(b), src, reads=list(r), writes=[b])
            self.issued += 1
        self.pos += 1
        return self.bufs[idx % self.n]


def build(nc, NG, NPG, dbg=False):
    NTOK = NG * G
    NPRE = NPG * G

    def dram(name, shape, dt=F32, kind="ExternalInput"):
        return nc.dram_tensor(name, list(shape), dt, kind=kind).ap()

    xm = dram("xm", [NTOK, D])
    xp = dram("xp", [NPRE, D])
    memd = dram("mem", [256, D])
    class DSrc:
        def __init__(self, name, shape, perblock=False):
            self.f = dram(name, shape)
            self.b = dram(name + "_b", shape, BF16, kind="Internal")
            self.buf = Buf(self.b, name)
            self.perblock = perblock
            self.bufs = [Buf(self.b, '%s_%d' % (name, i)) for i in range(shape[0])] if perblock else None

        def __getitem__(self, i):
            if self.perblock:
                return (self.b[i], self.bufs[i])
            return (self.b[i], self.buf)

    w1 = DSrc("w1", [NB1, 128, KC, 128])
    w2 = DSrc("w2", [NB2, 128, KC, 256], perblock=True)
    wal = dram("wal", [128, KC, 16])
    wco = DSrc("wco", [16, 128, 16, 128])
    wgo = DSrc("wgo", [16, 128, 16, 128])
    wmo = DSrc("wmo", [16, 128, 4, 128])
    wout = DSrc("wout", [8, 128, 16, 256])
    wmk = DSrc("wmk", [4, 128, KC, 128])
    wmv = DSrc("wmv", [2, 128, KC, 256])
    DSRCS = [w2, w1, wco, wgo, wmo, wout, wmk, wmv]
    cf = dram("cf", [128, 256])
    cbd = dram("cb", [128, 512])
    vecd = dram("vec", [128, NV])
    dwwd = dram("dww", [128, 16, 31])
    wa2d = dram("wa2", [33, 1024])
    bcod = DSrc("bco", [16, 1, 128])
    DSRCS.append(bcod)
    fgd = dram("fg", [D])
    outd = dram("out", [NTOK, D], kind="ExternalOutput")
    dgs = dram("dgs", [16, 2, 128, 16, 128], BF16, kind="Internal")
    if dbg:
        dbgd = dram("dbg", [16, 128, 512], kind="ExternalOutput")

    st = contextlib.ExitStack()
    S = Sched(nc, st)
    CF = S.sb([128, 256], F32, 'CF')
    CB = S.sb([128, 512], BF16, 'CB')
    VEC = S.sb([128, NV], F32, 'VEC')
    HB = S.sb([128, 48], F32, 'HB')
    HW = S.sb([128, 16, 31], F32, 'HW')
    FG = S.sb([128, D], F32, 'FG')
    WA2 = S.sb([33, 1024], BF16, 'WA2')
    ONES1 = S.sb([1, 512], BF16, 'ONES1')
    WAL = S.sb([128, KC, 16], BF16, 'WAL')
    HT = S.sb([128, KC, G], BF16, 'HT')
    HTH = S.sb([128, KC, 32], BF16, 'HTH')
    XB = [S.sb([128, D], F32, 'XB%d' % i) for i in range(2)]
    S32 = [S.sb([128, 512], F32, 'S32_%d' % i) for i in range(8)]
    SBF = [S.sb([128, 512], BF16, 'SBF_%d' % i) for i in range(8)]
    UH = S.sb([128, 16, 32], BF16, 'UH')
    MKT = S.sb([128, 4, 256], BF16, 'MKT')
    MV = S.sb([128, 2, 512], BF16, 'MV')
    ALT = S.sb([33, G], BF16, 'ALT')
    AL = S.sb([128, NT, 8], F32, 'AL')
    STT = [S.sb([128, 8], F32, 'ST%d' % i) for i in range(4)]
    DGD = Buf(dgs, 'dgs')
    W1 = WPool(S, 4, [128, 17, 128], BF16, 'W1', 'sp')
    W2 = WPool(S, 3, [128, KC, 256], BF16, 'W2', 'sp')
    DG = WPool(S, 2, [128, 16, 128], BF16, 'DG', 'sp')
    ARENA_BYTES = 76 * 1024
    arena = st.enter_context(nc.sbuf_tensor('arena', [128, ARENA_BYTES], U8))
    PSB = [S.ps([128, 512], F32, 'PS%d' % i) for i in range(8)]

    ident_f = CF[:, 0:128]
    mask_f = CF[:, 128:256]
    ident_b = CB[:, 0:128]
    tri_inc = CB[:, 128:256]
    tri_dec = CB[:, 256:384]
    ones_b = CB[:, 384:512]
    negcol = CB[:, 255:256]

    state = {'bank': 0, 'st': 0, 'apos': 0, 'evac': 0}

    def bank():
        b = PSB[state['bank'] % 6]
        state['bank'] += 1
        return b

    def stt_buf():
        b = STT[state['st'] % 4]
        state['st'] += 1
        return b

    def arena_reset(off=0):
        state['apos'] = off

    def carve(free_shape, dt, name='a'):
        esz = 4 if dt == F32 else 2
        n = int(np.prod(free_shape))
        nbytes = (n * esz + 31) // 32 * 32
        off = state['apos']
        assert off + nbytes <= ARENA_BYTES, (name, off, nbytes)
        state['apos'] = off + nbytes
        v = arena[:, off:off + n * esz].bitcast(dt)
        if len(free_shape) == 2:
            v = v.rearrange("p (a b) -> p a b", a=free_shape[0])
        elif len(free_shape) == 3:
            v = v.rearrange("p (a b c) -> p a b c", a=free_shape[0], b=free_shape[1])
        return Buf(v, name)

    def act(out, in_, func, R, Wr, **kw):
        S.op('act', 'activation', dict(out=out, in_=in_, func=func, **kw), R, Wr)

    def mm(out, lhsT, rhs, start, stop, R, Wr, sig=None):
        S.op('pe', 'matmul', dict(out=out, lhsT=lhsT, rhs=rhs, start=bool(start), stop=bool(stop)), R, Wr,
             signal=bool(stop) if sig is None else sig)

    def tr(out, in_, ident, R, Wr):
        S.op('pe', 'transpose', dict(out=out, in_=in_, identity=ident), R, Wr)

    def stt(out, in0, scalar, in1, op0, op1, R, Wr):
        S.op('dve', 'scalar_tensor_tensor', dict(out=out, in0=in0, scalar=scalar, in1=in1, op0=op0, op1=op1), R, Wr)

    def ts(out, in0, s1, op0, R, Wr, s2=None, op1=None, eng='dve'):
        kw = dict(out=out, in0=in0, scalar1=s1, scalar2=s2, op0=op0)
        if op1 is not None:
            kw['op1'] = op1
        S.op(eng, 'tensor_scalar', kw, R, Wr)

    def tt(out, in0, in1, op, R, Wr, eng='dve'):
        S.op(eng, 'tensor_tensor', dict(out=out, in0=in0, in1=in1, op=op), R, Wr)

    def cp(out, in_, R, Wr, eng='dve'):
        if eng == 'act':
            act(out, in_, AF.Copy, R, Wr)
        else:
            S.op(eng, 'tensor_copy', dict(out=out, in_=in_), R, Wr)

    def evac_scaled(out, in_, scal, R, Wr):
        state['evac'] += 1
        if state['evac'] % 2 == 0:
            act(out, in_, AF.Identity, R, Wr, scale=scal)
        else:
            ts(out, in_, scal, ALU.mult, R, Wr)

    def rstd_from_ss(stb, n):
        act(stb[:, 1:2], stb[:, 0:1], AF.Ln, [stb], [stb], scale=1.0 / n, bias=EPS)
        act(stb[:, 2:3], stb[:, 1:2], AF.Exp, [stb], [stb], scale=-0.5)

    def w1get(src, kcs=16, bias=None):
        parts = [((lambda b, kcs=kcs: b[:, 0:kcs, :]), src[0])]
        rds = [src[1]]
        if bias is not None:
            parts.append(((lambda b: b[0:1, 16, :]), bias[0]))
            rds.append(bias[1])
        return W1.get(parts, reads=rds)

    def w2get(src):
        return W2.get([((lambda b: b[:]), src[0])], reads=[src[1]])

    KVS = {'next': 0, 'n': 0}

    def pc_kv(upto):
        f2 = w2.f.rearrange("b p (h k) c -> (b p h) (k c)", h=2)
        b2 = w2.b.rearrange("b p (h k) c -> (b p h) (k c)", h=2)
        while KVS['next'] < min(upto, B2_Z):
            blk = KVS['next']
            KVS['next'] += 1
            for hh in range(2):
                a = blk * 256 + hh * 128
                k = KVS['n']
                KVS['n'] += 1
                bnc = W1.bufs[k % 4]
                flat = bnc[:, 0:16, :].rearrange("p k c -> p (k c)")
                S.dma('pool', 'pck%d' % (k % 4), flat, f2[a:a + 128], writes=[bnc])
                S.dma_mark('sp', 'pcs_w2_%d' % blk, b2[a:a + 128], flat, [bnc], w2.bufs[blk])

    def precast(ds, blk0=None, blk1=None):
        shp = ds.f.shape
        if len(shp) == 4:
            f2 = ds.f.rearrange("b p k c -> (b p) (k c)")
            b2 = ds.b.rearrange("b p k c -> (b p) (k c)")
        else:
            f2 = ds.f.rearrange("b p c -> (b p) c")
            b2 = ds.b.rearrange("b p c -> (b p) c")
        rpb = shp[1]
        if ds.perblock:
            for i in range(blk0, blk1):
                S.dma_nw('pool', 'pc_%s_%d' % (ds.buf.name, i), b2[i * rpb:(i + 1) * rpb], f2[i * rpb:(i + 1) * rpb], ds.bufs[i])
            return
        R = f2.shape[0]
        rows_per = 64 if f2.shape[1] >= 2048 else 256
        for a in range(0, R, rows_per):
            e = min(R, a + rows_per)
            S.dma_nw('pool', 'pc_%s' % ds.buf.name, b2[a:e], f2[a:e], ds.buf)

    def dump(i, buf, ap):
        if dbg:
            S.dma('pool', 'dbg', dbgd[i], ap, reads=[buf])

    def load_norm(src, row0, ntiles, gcol, dstbuf, dst_fn, junk):
        prep_tile(src, row0, 0, gcol, dstbuf, dst_fn, junk, 'a')
        for i in range(ntiles):
            if i + 1 < ntiles:
                prep_tile(src, row0, i + 1, gcol, dstbuf, dst_fn, junk, 'a')
            prep_tile(src, row0, i, gcol, dstbuf, dst_fn, junk, 'b')

    def prep_tile(src, row0, i, gcol, dstbuf, dst_fn, junk, part='ab'):
        xb = XB[i % 2]
        xs = junk.xs[i % 2]
        if 'a' in part:
            S.dma('sp', 'xb%d' % (i % 2), xb[:], src[row0 + i * 128: row0 + (i + 1) * 128, :], writes=[xb])
            sb_ = stt_buf()
            act(junk[:], xb[:], AF.Square, [xb], [junk, sb_], accum_out=sb_[:, 0:1])
            rstd_from_ss(sb_, D)
            ts(xs[:], xb[:], sb_[:, 2:3], ALU.mult, [xb, sb_], [xs])
        if 'b' in part:
            for q in range(2):
                pb = bank()
                pb16 = pb[:].bitcast(BF16)
                for j in range(8):
                    kc = q * 8 + j
                    tr(pb16[:, j * 128:(j + 1) * 128], xs[:, kc * 128:(kc + 1) * 128], ident_b, [xs, CB], [pb])
                for j in range(8):
                    kc = q * 8 + j
                    evac_scaled(dst_fn(kc, i), pb16[:, j * 128:(j + 1) * 128], VEC[:, gcol + kc:gcol + kc + 1], [pb, VEC], [dstbuf])

    def carve_junk(name='junk'):
        j = carve([D], BF16, name)
        j.xs = [carve([D], BF16, name + '_xs%d' % i) for i in range(2)]
        return j

    def alpha_proj(HT=HT):
        pb = bank()
        for kc in range(KC):
            mm(pb[0:16, :], WAL[:, kc, :], HT[:, kc, :], kc == 0, kc == KC - 1, [WAL, HT], [pb])
        cp(ALT[0:16, :], pb[0:16, :], [pb], [ALT], eng='act')

    def decay_tile(i, c0, ncol, E, L, E3, e3_fn):
        decay_half_A(i, c0, ncol, E, L)
        decay_half_B(i, c0, ncol, L, E3, e3_fn)

    def decay_half_A(i, c0, ncol, E, L):
        for c in range(0, ncol, 512):
            w = min(512, ncol - c)
            pb = bank()
            mm(pb[:, 0:w], ALT[0:33, i * 128:(i + 1) * 128], WA2[0:33, c0 + c:c0 + c + w], True, True, [ALT, WA2], [pb])
            act(E[:, c:c + w], pb[:, 0:w], AF.Exp, [pb], [E], scale=-1.0)
        act(L[:, 0:ncol], E[:, 0:ncol], AF.Ln, [E], [L], bias=1.0)

    def decay_half_B(i, c0, ncol, L, E3, e3_fn):
        for c in range(0, ncol, 512):
            w = min(512, ncol - c)
            pb = bank()
            mm(pb[:, 0:w], tri_dec, L[:, c:c + w], True, True, [CB, L], [pb])
            act(e3_fn(c, w), pb[:, 0:w], AF.Exp, [pb], [E3])
        pb = bank()
        nkb = ncol // 128
        for kb in range(nkb):
            mm(pb[:, kb:kb + 1], L[:, kb * 128:(kb + 1) * 128], negcol, True, True, [L, CB], [pb])
        kb0 = c0 // 128
        act(AL[:, i, kb0:kb0 + nkb], pb[:, 0:nkb], AF.Exp, [pb], [AL])

    def state_update(i, h, KDEC, kdec_fn, VT, vt_ap, to_bf):
        for kb2 in range(2):
            kb = 2 * h + kb2
            pb = bank()
            mm(pb[:], kdec_fn(i, kb2), vt_ap, True, True, [KDEC, VT], [pb])
            stt(S32[kb][:], S32[kb][:], AL[:, i, kb:kb + 1], pb[:], ALU.mult, ALU.add, [S32[kb], AL, pb], [S32[kb]])
            if to_bf:
                cp(SBF[kb][:], S32[kb][:], [S32[kb]], [SBF[kb]], eng='act')

    def prefix_bufs():
        arena_reset(0)
        junk = carve_junk('junk')
        E = carve([1024], F32, 'E')
        Ls = [carve([1024], BF16, 'L%d' % i) for i in range(2)]
        E3 = carve([NT, 1024], F32, 'E3')
        KDEC = carve([NT, 1024], BF16, 'KDEC')
        VT = carve([NT, D], BF16, 'VT')
        HT2 = carve([KC, G], BF16, 'HT2')
        return junk, E, Ls, E3, KDEC, VT, HT2

    def prefix_group(g, last, pbufs, HT_main=HT):
        junk, E, Ls, E3, KDEC, VT, HT2 = pbufs
        mark('p%d' % g)
        HT = HT_main if g % 2 == 0 else HT2
        HTn = HT2 if g % 2 == 0 else HT_main

        def prep_next(i, part='ab'):
            if not last:
                prep_tile(xp, (g + 1) * G, i, V_ING, HTn, lambda kc, i: HTn[:, kc, i * 128:(i + 1) * 128], junk, part)

        if g == 0:
            load_norm(xp, 0, NT, V_ING, HT, lambda kc, i: HT[:, kc, i * 128:(i + 1) * 128], junk)
        if last:
            cp(HTH[:], HT[:, :, G - 32:G], [HT], [HTH], eng='dve')
        alpha_proj(HT)

        def k_block(nb):
            w = w2get(w2[B2_K + nb])
            for i in range(NT):
                pb = bank()
                for kc in range(KC):
                    mm(pb[:, 0:256], HT[:, kc, i * 128:(i + 1) * 128], w[:, kc, :], kc == 0, kc == KC - 1, [HT, w], [pb])
                tt(KDEC[:, i, nb * 256:(nb + 1) * 256], pb[:, 0:256], E3[:, i, nb * 256:(nb + 1) * 256], ALU.mult, [pb, E3], [KDEC])

        def v_block(nb):
            w = w2get(w2[B2_V + nb])
            for i in range(NT):
                pb = bank()
                for kc in range(KC):
                    mm(pb[:, 0:256], HT[:, kc, i * 128:(i + 1) * 128], w[:, kc, :], kc == 0, kc == KC - 1, [HT, w], [pb])
                cp(VT[:, i, nb * 256:(nb + 1) * 256], pb[:, 0:256], [pb], [VT], eng=('act' if (nb + i) % 2 else 'dve'))

        if g == 0:
            for i in range(NT):
                decay_tile(i, 0, 1024, E, Ls[i % 2], E3, lambda c, w, i=i: E3[:, i, c:c + w])
            for nb in range(4):
                if nb == 1:
                    prep_next(0, 'a')
                if nb == 2:
                    prep_next(0, 'b')
                pc_kv(B2_K + nb + 3)
                k_block(nb)
            for nb in range(8):
                if nb in (0, 3, 6):
                    prep_next({0: 1, 3: 2, 6: 3}[nb], 'a')
                if nb in (1, 4, 7):
                    prep_next({1: 1, 4: 2, 7: 3}[nb], 'b')
                pc_kv(B2_V + nb + 3)
                v_block(nb)
        else:
            for nb in range(8):
                if nb < NT:
                    decay_half_A(nb, 0, 1024, E, Ls[nb % 2])
                if nb in (1, 4, 7):
                    prep_next({1: 0, 4: 1, 7: 2}[nb], 'a')
                if nb in (2, 5):
                    prep_next({2: 0, 5: 1}[nb], 'b')
                v_block(nb)
                if nb < NT:
                    decay_half_B(nb, 0, 1024, Ls[nb % 2], E3, lambda c, w, i=nb: E3[:, i, c:c + w])
            for nb in range(4):
                if nb == 1:
                    prep_next(3, 'a')
                if nb in (0, 2):
                    prep_next({0: 2, 2: 3}[nb], 'b')
                k_block(nb)
        if g == 0:
            pc_kv(B2_Z)
            background_precast()
        for i in range(NT):
            for h in range(4):
                state_update(i, h, KDEC, lambda i, kb2, h=h: KDEC[:, i, (2 * h + kb2) * 128:(2 * h + kb2 + 1) * 128],
                             VT, VT[:, i, h * 512:(h + 1) * 512], False)

    def startup():
        S.dma('sp', 'c0', CF[:], cf[:, :], writes=[CF])
        S.dma('sp', 'c0', VEC[:], vecd[:, :], writes=[VEC])
        S.dma('sp', 'c0', HW[:], dwwd[:, :, :], writes=[HW])
        S.dma('sp', 'c0', FG[:], fgd.partition_broadcast(128), writes=[FG])
        S.dma('pool', 'c1', CB[:], cbd[:, :], writes=[CB])
        S.dma('pool', 'c1', WA2[:], wa2d[:, :], writes=[WA2])
        S.dma('pool', 'c1', WAL[:], wal[:, :, :], writes=[WAL])
        precast(bcod)

    def background_precast():
        precast(wmk)
        precast(wmv)
        precast(w1)
        precast(w2, B2_Z, NB2)
        for ds in (wco, wgo, wmo, wout):
            precast(ds)
        S.op('dve', 'memset', dict(ap=ONES1[:], constant=1.0), [], [ONES1])
        S.op('dve', 'memset', dict(ap=ALT[:], constant=0.0), [], [ALT])
        S.op('dve', 'memset', dict(ap=ALT[32:33, :], constant=1.0), [], [ALT])
        for kb in range(8):
            S.op('dve', 'memset', dict(ap=S32[kb][:], constant=0.0), [], [S32[kb]])
        ts(HB[:], VEC[:, V_BG:V_BG + 48], 0.5, ALU.mult, [VEC], [HB])
        ts(HW[:], HW[:], 0.5, ALU.mult, [HW], [HW])

    def dg_build():
        for cbk in range(16):
            for half in range(2):
                b = DG.bufs[(cbk * 2 + half) % 2]
                ntap = 16 if half == 0 else 15
                for jj in range(ntap):
                    j = half * 16 + jj
                    ts(b[:, jj, :], ident_f, HW[:, cbk, j:j + 1], ALU.mult, [CF, HW], [b])
                if half == 1:
                    S.op('dve', 'memset', dict(ap=b[:, 15, :], constant=0.0), [], [b])
                S.dma('sp', 'dgw', dgs[cbk, half], b[:], reads=[b], writes=[DGD])

    def mem_kv():
        arena_reset(0)
        junk = carve_junk('junk')
        MT = carve([KC, 256], BF16, 'MT')
        load_norm(memd, 0, 2, V_MLG, MT, lambda kc, i: MT[:, kc, i * 128:(i + 1) * 128], junk)
        for h in range(4):
            w = w1get(wmk[h])
            pb = bank()
            for kc in range(KC):
                mm(pb[:, 0:256], w[:, kc, :], MT[:, kc, :], kc == 0, kc == KC - 1, [w, MT], [pb])
            cp(MKT[:, h, :], pb[:, 0:256], [pb], [MKT], eng='act')
        for nb in range(2):
            w = w2get(wmv[nb])
            for mb in range(2):
                pb = bank()
                for kc in range(KC):
                    mm(pb[:, 0:256], MT[:, kc, mb * 128:(mb + 1) * 128], w[:, kc, :], kc == 0, kc == KC - 1, [MT, w], [pb])
                cp(MV[:, mb, nb * 256:(nb + 1) * 256], pb[:, 0:256], [pb], [MV], eng='dve')
        S.barrier()

    def halo_stage():
        for kb in range(8):
            cp(SBF[kb][:], S32[kb][:], [S32[kb]], [SBF[kb]], eng='act')
        S.barrier()

    def main_group(g):
        arena_reset(0)
        UF = carve([16, G], BF16, 'UF')
        OZ = carve([16, G], BF16, 'OZ')
        OMZ = carve([4, G], BF16, 'OMZ')
        base = state['apos']

        mark('g%d_s0' % g)
        junk = carve_junk('junk')
        load_norm(xm, g * G, NT, V_ING, HT, lambda kc, i: HT[:, kc, i * 128:(i + 1) * 128], junk)
        dump(0, HT, HT[:, 0, :])
        S.barrier()

        mark('g%d_s1' % g)
        arena_reset(base)
        YC = carve([16, G], BF16, 'YC')
        Us = [carve([32 + G], BF16, 'U%d' % i) for i in range(2)]
        YSQs = [carve([G], BF16, 'YSQ%d' % i) for i in range(2)]
        THBs = [carve([G], F32, 'THB%d' % i) for i in range(2)]
        T1s = [carve([G], F32, 'T1_%d' % i) for i in range(2)]
        T2s = [carve([G], F32, 'T2_%d' % i) for i in range(2)]
        MU = carve([G], F32, 'MU')
        MSQ = carve([G], F32, 'MSQ')
        P6, P7 = PSB[6], PSB[7]
        THBh = [carve([32], F32, 'THBh%d' % i) for i in range(2)]
        for cbk in range(16):
            wa = w1get(w1[B1_A + cbk])
            pa = bank()
            for kc in range(KC):
                mm(pa[:], wa[:, kc, :], HT[:, kc, :], kc == 0, kc == KC - 1, [wa, HT], [pa])
            if g == 0:
                ph = bank()
                for kc in range(KC):
                    mm(ph[:, 0:32], wa[:, kc, :], HTH[:, kc, :], kc == 0, kc == KC - 1, [wa, HTH], [ph])
            wb = w1get(w1[B1_B + cbk])
            pb = bank()
            for kc in range(KC):
                mm(pb[:], wb[:, kc, :], HT[:, kc, :], kc == 0, kc == KC - 1, [wb, HT], [pb])
            if g == 0:
                for kc in range(KC):
                    mm(ph[:, 32:64], wb[:, kc, :], HTH[:, kc, :], kc == 0, kc == KC - 1, [wb, HTH], [ph])
                thh = THBh[cbk % 2]
                act(thh[:], ph[:, 32:64], AF.Tanh, [ph], [thh], scale=0.5)
                stt(UH[:, cbk, :], thh[:], 1.0, ph[:, 0:32], ALU.add, ALU.mult, [thh, ph], [UH])
            wz = w1get(w1[B1_Z + cbk])
            pz = bank()
            for kc in range(KC):
                mm(pz[:], wz[:, kc, :], HT[:, kc, :], kc == 0, kc == KC - 1, [wz, HT], [pz])
            thb = THBs[cbk % 2]
            U = Us[cbk % 2]
            act(thb[:], pb[:], AF.Tanh, [pb], [thb], scale=0.5)
            cp(U[:, 0:32], UH[:, cbk, :], [UH], [U], eng='dve')
            stt(U[:, 32:32 + G], thb[:], 1.0, pa[:], ALU.add, ALU.mult, [thb, pa], [U])
            cp(UH[:, cbk, :], U[:, G:G + 32], [U], [UH], eng='dve')
            act(UF[:, cbk, :], pz[:], AF.Silu, [pz], [UF])
            pc = bank()
            dgx = DG.get([((lambda b: b[:]), dgs[cbk, 0])], reads=[DGD])
            for j in range(16):
                mm(pc[:], dgx[:, j, :], U[:, 2 + j:2 + j + G], j == 0, False, [dgx, U], [pc])
            dgx = DG.get([((lambda b: b[:]), dgs[cbk, 1])], reads=[DGD])
            for j in range(16, 31):
                mm(pc[:], dgx[:, j - 16, :], U[:, 2 + j:2 + j + G], False, j == 30, [dgx, U], [pc])
            ysq = YSQs[cbk % 2]
            act(YC[:, cbk, :], pc[:], AF.Identity, [pc, VEC], [YC], bias=VEC[:, V_DWB + cbk:V_DWB + cbk + 1])
            act(ysq[:], pc[:], AF.Square, [pc, VEC], [ysq], bias=VEC[:, V_DWB + cbk:V_DWB + cbk + 1])
            mm(P6[:], ones_b, YC[:, cbk, :], cbk == 0, cbk == 15, [CB, YC], [P6], sig=True)
            mm(P7[:], ones_b, ysq[:], cbk == 0, cbk == 15, [CB, ysq], [P7], sig=True)
        mark('g%d_s1ln' % g)
        act(MU[:], P6[:], AF.Copy, [P6], [MU], scale=1.0 / D)
        tt(MSQ[:], MU[:], MU[:], ALU.mult, [MU], [MSQ])
        stt(MSQ[:], P7[:], 1.0 / D, MSQ[:], ALU.mult, ALU.subtract, [P7, MSQ], [MSQ])
        ts(MSQ[:], MSQ[:], 0.0, ALU.max, [MSQ], [MSQ])
        act(MSQ[:], MSQ[:], AF.Ln, [MSQ], [MSQ], bias=EPS)
        act(P6[:], MSQ[:], AF.Exp, [MSQ], [P6], scale=-0.5)
        stt(P7[:], MU[:], -1.0, P6[:], ALU.mult, ALU.mult, [MU, P6], [P7])
        for cbk in range(16):
            t1 = T1s[cbk % 2]
            t2 = T2s[cbk % 2]
            tt(t1[:], YC[:, cbk, :], P6[:], ALU.mult, [YC, P6], [t1])
            tt(t1[:], t1[:], P7[:], ALU.add, [t1, P7], [t1])
            act(t2[:], t1[:], AF.Silu, [t1, VEC], [t2], scale=VEC[:, V_LNG + cbk:V_LNG + cbk + 1], bias=VEC[:, V_LNB + cbk:V_LNB + cbk + 1])
            tt(UF[:, cbk, :], t2[:], UF[:, cbk, :], ALU.mult, [t2, UF], [UF], eng='pool')
        dump(1, UF, UF[:, 0, :])
        dump(5, YC, YC[:, 0, :])
        S.barrier()

        mark('g%d_s3' % g)
        arena_reset(base)
        junk = carve([512], BF16, 'junk')
        E = carve([256], F32, 'E')
        Lh = carve([NT, 256], BF16, 'Lh')
        E3 = carve([NT, 256], F32, 'E3')
        E1s = [carve([G], F32, 'E1_%d' % i) for i in range(2)]
        E2s = [carve([G], F32, 'E2_%d' % i) for i in range(2)]
        QE = carve([2, G], BF16, 'QE')
        KD = carve([2, G], BF16, 'KD')
        KDEC = carve([NT, 256], BF16, 'KDEC')
        VT = carve([NT, 512], BF16, 'VT')
        ZT = carve([NT, 512], BF16, 'ZT')
        PTs = [carve([128], BF16, 'PT%d' % i) for i in range(2)]
        OTs = [carve([512], F32, 'OT%d' % i) for i in range(2)]
        alpha_proj()
        for h in range(4):
            for i in range(NT):
                _decay_tile_h(i, h, E, Lh, E3)
            for kb2 in range(2):
                pbf = bank()
                for i in range(NT):
                    mm(pbf[:, i * 128:(i + 1) * 128], Lh[:, i, kb2 * 128:(kb2 + 1) * 128], tri_inc, True, True, [Lh, CB], [pbf])
                e1 = E1s[kb2]
                e2 = E2s[kb2]
                act(e1[:], pbf[:], AF.Exp, [pbf, LN16], [e1], bias=LN16[:, 0:1])
                act(e2[:], pbf[:], AF.Exp, [pbf], [e2], scale=-1.0)
                wq = w1get(w1[B1_Q + 2 * h + kb2])
                pq = bank()
                for kc in range(KC):
                    mm(pq[:], wq[:, kc, :], HT[:, kc, :], kc == 0, kc == KC - 1, [wq, HT], [pq])
                tt(QE[:, kb2, :], pq[:], e1[:], ALU.mult, [pq, e1], [QE])
                wk = w1get(w1[B1_K + 2 * h + kb2])
                pk = bank()
                for kc in range(KC):
                    mm(pk[:], wk[:, kc, :], HT[:, kc, :], kc == 0, kc == KC - 1, [wk, HT], [pk])
                tt(KD[:, kb2, :], pk[:], e2[:], ALU.mult, [pk, e2], [KD])
            w = w2get(w2[B2_K + h])
            for i in range(NT):
                pb = bank()
                for kc in range(KC):
                    mm(pb[:, 0:256], HT[:, kc, i * 128:(i + 1) * 128], w[:, kc, :], kc == 0, kc == KC - 1, [HT, w], [pb])
                tt(KDEC[:, i, :], pb[:, 0:256], E3[:, i, :], ALU.mult, [pb, E3], [KDEC])
            for nb2 in range(2):
                w = w2get(w2[B2_V + 2 * h + nb2])
                for i in range(NT):
                    pb = bank()
                    for kc in range(KC):
                        mm(pb[:, 0:256], HT[:, kc, i * 128:(i + 1) * 128], w[:, kc, :], kc == 0, kc == KC - 1, [HT, w], [pb])
                    cp(VT[:, i, nb2 * 256:(nb2 + 1) * 256], pb[:, 0:256], [pb], [VT], eng=('act' if i % 2 else 'dve'))
            for nb2 in range(2):
                w = w2get(w2[B2_Z + 2 * h + nb2])
                for i in range(NT):
                    pb = bank()
                    for kc in range(KC):
                        mm(pb[:, 0:256], HT[:, kc, i * 128:(i + 1) * 128], w[:, kc, :], kc == 0, kc == KC - 1, [HT, w], [pb])
                    act(ZT[:, i, nb2 * 256:(nb2 + 1) * 256], pb[:, 0:256], AF.Silu, [pb], [ZT])
            for i in range(NT):
                tsl = slice(i * 128, (i + 1) * 128)
                psc = bank()
                for kb2 in range(2):
                    mm(psc[:, 0:128], KD[:, kb2, tsl], QE[:, kb2, tsl], kb2 == 0, kb2 == 1, [KD, QE], [psc])
                pt = PTs[i % 2]
                tt(pt[:], psc[:, 0:128], mask_f, ALU.mult, [psc, CF], [pt])
                po = bank()
                mm(po[:], QE[:, 0, tsl], SBF[2 * h][:], True, False, [QE, SBF[2 * h]], [po])
                mm(po[:], QE[:, 1, tsl], SBF[2 * h + 1][:], False, False, [QE, SBF[2 * h + 1]], [po])
                mm(po[:], pt[:], VT[:, i, :], False, True, [pt, VT], [po])
                state_update(i, h, KDEC, lambda i, kb2: KDEC[:, i, kb2 * 128:(kb2 + 1) * 128], VT, VT[:, i, :], True)
                sb_ = stt_buf()
                act(junk[:], po[:], AF.Square, [po], [junk, sb_], accum_out=sb_[:, 0:1])
                rstd_from_ss(sb_, 512)
                ot = OTs[i % 2]
                stt(ot[:], po[:], sb_[:, 2:3], ZT[:, i, :], ALU.mult, ALU.mult, [po, sb_, ZT], [ot])
                ptb = bank()
                for vb in range(4):
                    tr(ptb[:, vb * 128:(vb + 1) * 128], ot[:, vb * 128:(vb + 1) * 128], ident_f, [ot, CF], [ptb])
                for vb in range(4):
                    c = 4 * h + vb
                    evac_scaled(OZ[:, c, tsl], ptb[:, vb * 128:(vb + 1) * 128], VEC[:, V_GNG + c:V_GNG + c + 1], [ptb, VEC], [OZ])
        dump(2, OZ, OZ[:, 0, :])
        dump(6, OZ, OZ[:, 4, :])
        S.barrier()

        mark('g%d_s4' % g)
        arena_reset(base)
        MQ = carve([4, G], BF16, 'MQ')
        SMZ = carve([4, G], BF16, 'SMZ')
        Ps = [carve([256], F32, 'P%d' % i) for i in range(2)]
        PNs = [carve([256], BF16, 'PN%d' % i) for i in range(2)]
        PTm = [carve([256], BF16, 'PTm%d' % i) for i in range(2)]
        for h in range(4):
            w = w1get(w1[B1_MQ + h])
            pb = bank()
            for kc in range(KC):
                mm(pb[:], w[:, kc, :], HT[:, kc, :], kc == 0, kc == KC - 1, [w, HT], [pb])
            act(MQ[:, h, :], pb[:], AF.Copy, [pb], [MQ], scale=float(128 ** -0.5))
            w = w1get(w1[B1_MZ + h])
            pb = bank()
            for kc in range(KC):
                mm(pb[:], w[:, kc, :], HT[:, kc, :], kc == 0, kc == KC - 1, [w, HT], [pb])
            act(SMZ[:, h, :], pb[:], AF.Silu, [pb], [SMZ])
        its = [(i, h) for i in range(NT) for h in range(4)]

        def st_A(n):
            i, h = its[n]
            tsl = slice(i * 128, (i + 1) * 128)
            psc = bank()
            mm(psc[:, 0:256], MQ[:, h, tsl], MKT[:, h, :], True, True, [MQ, MKT], [psc])
            sb_ = stt_buf()
            S.op('dve', 'tensor_reduce', dict(out=sb_[:, 0:1], in_=psc[:, 0:256], axis=AX.X, op=ALU.max), [psc], [sb_])
            ts(sb_[:, 1:2], sb_[:, 0:1], -1.0, ALU.mult, [sb_], [sb_])
            P = Ps[n % 2]
            PN = PNs[n % 2]
            act(P[:], psc[:, 0:256], AF.Exp, [psc, sb_], [P, sb_], bias=sb_[:, 1:2], accum_out=sb_[:, 2:3])
            S.op('dve', 'reciprocal', dict(out=sb_[:, 3:4], in_=sb_[:, 2:3]), [sb_], [sb_])
            ts(PN[:], P[:], sb_[:, 3:4], ALU.mult, [P, sb_], [PN])

        def st_B(n):
            PN = PNs[n % 2]
            PT = PTm[n % 2]
            ptb = bank()
            ptb16 = ptb[:].bitcast(BF16)
            for mb in range(2):
                tr(ptb16[:, mb * 128:(mb + 1) * 128], PN[:, mb * 128:(mb + 1) * 128], ident_b, [PN, CB], [ptb])
            cp(PT[:], ptb16[:, 0:256], [ptb], [PT], eng='act')

        def st_C(n):
            i, h = its[n]
            tsl = slice(i * 128, (i + 1) * 128)
            PT = PTm[n % 2]
            pom = bank()
            for mb in range(2):
                mm(pom[:, 0:128], MV[:, mb, h * 128:(h + 1) * 128], PT[:, mb * 128:(mb + 1) * 128], mb == 0, mb == 1, [MV, PT], [pom])
            tt(OMZ[:, h, tsl], pom[:, 0:128], SMZ[:, h, tsl], ALU.mult, [pom, SMZ], [OMZ])

        NI = len(its)
        for n in range(NI + 2):
            if n < NI:
                st_A(n)
            if n >= 2:
                st_C(n - 2)
            if 1 <= n <= NI:
                st_B(n - 1)
        dump(3, OMZ, OMZ[:, 0, :])
        S.barrier()

        mark('g%d_s5' % g)
        arena_reset(base)
        YS = carve([16, G], BF16, 'YS')
        THs = [carve([G], F32, 'TH%d' % i) for i in range(2)]
        THs.append(THs[0])
        Ms = [carve([G], F32, 'M%d' % i) for i in range(2)]
        Ms.append(Ms[1])
        for db in range(16):
            for br in range(3):
                if br == 0:
                    wy = w1get(wco[db], bias=bcod[db])
                    src, nk = UF, 16
                elif br == 1:
                    wy = w1get(wgo[db])
                    src, nk = OZ, 16
                else:
                    wy = w1get(wmo[db], kcs=4)
                    src, nk = OMZ, 4
                py = bank()
                for kc in range(nk):
                    mm(py[:], wy[:, kc, :], src[:, kc, :], kc == 0, (kc == nk - 1) and br != 0, [wy, src], [py])
                if br == 0:
                    mm(py[:], wy[0:1, 16, :], ONES1[0:1, :], False, True, [wy, ONES1], [py])
                wg = w1get(w1[B1_G + br * 16 + db])
                pg = bank()
                for kc in range(KC):
                    mm(pg[:], wg[:, kc, :], HT[:, kc, :], kc == 0, kc == KC - 1, [wg, HT], [pg])
                th = THs[br]
                act(th[:], pg[:], AF.Tanh, [pg, HB], [th], scale=0.5, bias=HB[:, br * 16 + db:br * 16 + db + 1])
                stt(Ms[br][:], th[:], 1.0, py[:], ALU.add, ALU.mult, [th, py], [Ms[br]])
                if br == 1:
                    tt(Ms[0][:], Ms[0][:], Ms[1][:], ALU.add, [Ms[0], Ms[1]], [Ms[0]])
            tt(YS[:, db, :], Ms[0][:], Ms[2][:], ALU.add, [Ms[0], Ms[2]], [YS])
        dump(4, YS, YS[:, 0, :])
        mark('g%d_s5out' % g)
        XBs = [XB[0], XB[1], carve([D], F32, 'XB2'), carve([D], F32, 'XB3')]
        for i in range(NT):
            S.dma('sp', 'xo%d' % i, XBs[i][:], xm[g * G + i * 128:g * G + (i + 1) * 128, :], writes=[XBs[i]])
        for nb in range(8):
            w = w2get(wout[nb])
            for i in range(NT):
                xb = XBs[i]
                pb = bank()
                for db in range(16):
                    mm(pb[:, 0:256], YS[:, db, i * 128:(i + 1) * 128], w[:, db, :], db == 0, db == 15, [YS, w], [pb])
                stt(xb[:, nb * 256:(nb + 1) * 256], pb[:, 0:256], 0.5, xb[:, nb * 256:(nb + 1) * 256], ALU.mult, ALU.add, [pb, xb], [xb])
        junkv = YS[:, 0:4, :]
        for i in range(NT):
            xb = XBs[i]
            sb_ = stt_buf()
            act(junkv, xb[:].rearrange("p (a b) -> p a b", a=4), AF.Square, [xb], [YS, sb_], accum_out=sb_[:, 0:1])
            rstd_from_ss(sb_, D)
            stt(xb[:], xb[:], sb_[:, 2:3], FG[:], ALU.mult, ALU.mult, [xb, sb_, FG], [xb])
            S.dma('sp', 'out', outd[g * G + i * 128:g * G + (i + 1) * 128, :], xb[:], reads=[xb])
        S.barrier()

    def _decay_tile_h(i, h, E, Lh, E3):
        c0 = h * 256
        pb = bank()
        mm(pb[:, 0:256], ALT[0:33, i * 128:(i + 1) * 128], WA2[0:33, c0:c0 + 256], True, True, [ALT, WA2], [pb])
        act(E[:], pb[:, 0:256], AF.Exp, [pb], [E], scale=-1.0)
        act(Lh[:, i, :], E[:], AF.Ln, [E], [Lh], bias=1.0)
        pb = bank()
        mm(pb[:, 0:256], tri_dec, Lh[:, i, :], True, True, [CB, Lh], [pb])
        act(E3[:, i, :], pb[:, 0:256], AF.Exp, [pb], [E3])
        pb = bank()
        for kb2 in range(2):
            mm(pb[:, kb2:kb2 + 1], Lh[:, i, kb2 * 128:(kb2 + 1) * 128], negcol, True, True, [Lh, CB], [pb])
        act(AL[:, i, 2 * h:2 * h + 2], pb[:, 0:2], AF.Exp, [pb], [AL])

    LN16 = S.sb([128, 1], F32, 'LN16')

    def mark(name):
        if not S.dry:
            MARKS.append((name, sum(1 for it in S.prog['pe'] if it[0] == 'inst')))

    def emit_all():
        S.op('dve', 'memset', dict(ap=LN16[:], constant=float(np.log(1.0 / 16.0))), [], [LN16])
        startup()
        pbufs = prefix_bufs()
        KVS['next'] = 0
        KVS['n'] = 0
        pc_kv(2)
        for g in range(NPG):
            prefix_group(g, g == NPG - 1, pbufs)
            if g == 0:
                dg_build()
        if NPG == 0:
            dg_build()
        S.barrier()
        mark('memkv')
        mem_kv()
        mark('halo')
        halo_stage()
        for g in range(NG):
            main_group(g)

    S.dry = True
    emit_all()
    S.dry = False
    S.reset()
    state.update({'bank': 0, 'st': 0, 'apos': 0, 'evac': 0})
    for p in (W1, W2, DG):
        p.start_real()
    allb = [CF, CB, VEC, HB, HW, FG, WA2, ONES1, WAL, HT, HTH, UH, MKT, MV, ALT, AL, DGD, LN16] + XB + S32 + SBF + STT + PSB
    for b in allb:
        b.lw = None
        b.rd = {}
    emit_all()
    mark('end')
    S.final_wait('sp', 'out')
    if dbg:
        S.final_wait('pool', 'dbg')
    S.emit()
    st.close()
    return nc


_CACHE = {}
MARKS = []


def _host_consts():
    ident = np.eye(128, dtype=np.float32)
    s = np.arange(128)[:, None]
    t = np.arange(128)[None, :]
    mask = (s <= t).astype(np.float32)
    tri_inc = np.where(s <= t, -1.0 / 16.0, 0.0).astype(np.float32)
    tri_dec = np.where(s > t, -1.0 / 16.0, 0.0).astype(np.float32)
    ones = np.ones((128, 128), np.float32)
    cf = np.concatenate([ident, mask], axis=1)
    cb = np.concatenate([ident, tri_inc, tri_dec, ones], axis=1)
    return np.ascontiguousarray(cf), np.ascontiguousarray(cb)


def _blk1(w, ncols_blk=128):
    K, N = w.shape
    return np.ascontiguousarray(w.reshape(K // 128, 128, N // ncols_blk, ncols_blk).transpose(2, 1, 0, 3))


def _pvec(v):
    return np.ascontiguousarray(v.reshape(-1, 128).T)


def prep_weights(inp):
    w_in = np.asarray(inp["w_in"])[0]
    cols1 = np.concatenate([w_in[:, C_A:C_A + 2048], w_in[:, C_B:C_B + 2048], w_in[:, C_Z:C_Z + 2048],
                            w_in[:, C_Q:C_Q + 1024], w_in[:, C_K:C_K + 1024], w_in[:, C_MQ:C_MQ + 512],
                            w_in[:, C_MZ:C_MZ + 512], w_in[:, C_GATE:C_GATE + 6144]], axis=1)
    cols2 = np.concatenate([w_in[:, C_K:C_K + 1024], w_in[:, C_V:C_V + 2048], w_in[:, C_GZ:C_GZ + 2048]], axis=1)
    cf, cb = _host_consts()
    vec = np.concatenate([_pvec(np.asarray(inp["b_gate"])[0]), _pvec(np.asarray(inp["dw_b"])[0]),
                          _pvec(np.asarray(inp["conv_ln_g"])[0]), _pvec(np.asarray(inp["conv_ln_b"])[0]),
                          _pvec(np.asarray(inp["ln_in_g"])[0]), _pvec(np.asarray(inp["gla_norm_g"])[0]),
                          _pvec(np.asarray(inp["mem_ln_g"])[0])], axis=1)
    dw = np.asarray(inp["dw_w"])[0]
    dww = np.ascontiguousarray(dw.reshape(31, 16, 128).transpose(2, 1, 0))
    wa2 = np.zeros((33, 1024), np.float32)
    wa2[0:16] = np.asarray(inp["w_alpha2"])[0]
    wa2[32] = np.asarray(inp["b_alpha"])[0]
    wkv = np.asarray(inp["w_mem_kv"])[0]
    d = {
        "w1": _blk1(cols1), "w2": _blk1(cols2, 256),
        "wal": np.ascontiguousarray(w_in[:, C_AL:C_AL + 16].reshape(16, 128, 16).transpose(1, 0, 2)),
        "wco": _blk1(np.asarray(inp["w_conv_out"])[0]), "wgo": _blk1(np.asarray(inp["w_gla_out"])[0]),
        "wmo": _blk1(np.asarray(inp["w_mem_out"])[0]), "wout": _blk1(np.asarray(inp["w_out"])[0], 256),
        "wmk": _blk1(wkv[:, 0:512]), "wmv": _blk1(wkv[:, 512:1024], 256),
        "cf": cf, "cb": cb, "vec": np.ascontiguousarray(vec.astype(np.float32)), "dww": dww.astype(np.float32),
        "wa2": wa2, "bco": np.ascontiguousarray(np.asarray(inp["b_conv_out"])[0].reshape(16, 1, 128)),
        "fg": np.ascontiguousarray(np.asarray(inp["final_g"]).astype(np.float32)),
    }
    return d


def run(inp, NG=NG_FULL, NPG=NPG_FULL, cores=N_CORES, dbg=False):
    key = (NG, NPG, dbg)
    nc = bass.Bass("TRN2", target_bir_lowering=False)
    build(nc, NG, NPG, dbg)
    wd = prep_weights(inp)
    x = np.asarray(inp["x"])
    mem = np.asarray(inp["mem"])
    SEG = x.shape[1] // 4
    in_maps = []
    for c in range(cores):
        b, j = c // 4, c % 4
        s0 = j * SEG
        xm = np.ascontiguousarray(x[b, s0:s0 + NG * G])
        npre = NPG * G
        xp = np.zeros((npre, D), np.float32)
        take = min(npre, s0)
        if take > 0:
            xp[npre - take:] = x[b, s0 - take:s0]
        m = dict(wd)
        m["xm"] = xm
        m["xp"] = xp
        m["mem"] = np.ascontiguousarray(mem[b])
        in_maps.append(m)
    res = run_bass_kernel_spmd(nc, in_maps, core_ids=list(range(cores)))
    return res


def kernel(**inputs):
    res = run(inputs)
    x = np.asarray(inputs["x"])
    out = np.empty(x.shape, np.float32)
    SEG = x.shape[1] // 4
    for c in range(N_CORES):
        b, j = c // 4, c % 4
        out[b, j * SEG:(j + 1) * SEG] = res.results[c]["out"]
    return out
```

```python
import contextlib
import numpy as np
import concourse.bass as bass
import concourse.mybir as mybir
from concourse.bass_utils import run_bass_kernel_spmd

F32 = mybir.dt.float32
BF16 = mybir.dt.bfloat16
U8 = mybir.dt.uint8
AF = mybir.ActivationFunctionType
ALU = mybir.AluOpType
AX = mybir.AxisListType

D = 2048
KC = 16
G = 512
NT = 4
EPS = 1e-6
N_CORES = 8
NG_FULL = 4
NPG_FULL = 12

C_A, C_B, C_Z, C_Q, C_K, C_V, C_GZ, C_AL, C_MQ, C_MZ, C_GATE = 0, 2048, 4096, 6144, 7168, 8192, 10240, 12288, 12304, 12816, 13328
B1_A, B1_B, B1_Z, B1_Q, B1_K, B1_MQ, B1_MZ, B1_G = 0, 16, 32, 48, 56, 64, 68, 72
NB1 = 120
B2_K, B2_V, B2_Z = 0, 4, 12
NB2 = 20
V_BG, V_DWB, V_LNG, V_LNB, V_ING, V_GNG, V_MLG, NV = 0, 48, 64, 80, 96, 112, 128, 144


class Buf:
    def __init__(self, t, name=''):
        self.t = t
        self.name = name
        self.lw = None
        self.rd = {}

    def __getitem__(self, idx):
        return self.t[idx]


class Sched:
    ENG = ['pe', 'act', 'dve', 'pool', 'sp']

    def __init__(self, nc, st, same_sync=('act', 'dve', 'pool')):
        self.nc = nc
        self.st = st
        self.sem = {n: st.enter_context(nc.semaphore('s_' + n)) for n in self.ENG}
        self.dsem_h = {}
        self.same_sync = set(same_sync)
        self.nsb = 0
        self.dry = False
        self.reset()

    def reset(self):
        self.prog = {n: [] for n in self.ENG}
        self.cnt = {n: 0 for n in self.ENG}
        self.waited = {n: {} for n in self.ENG}
        self.dcnt = {}
        self.pend = {n: False for n in self.ENG}
        self.bufs = []

    def track(self, b):
        self.bufs.append(b)
        return b

    def sb(self, shape, dt, name=None):
        self.nsb += 1
        name = name or ('sb%d' % self.nsb)
        return Buf(self.st.enter_context(self.nc.sbuf_tensor(name, list(shape), dt)), name)

    def ps(self, shape, dt, name=None):
        self.nsb += 1
        name = name or ('ps%d' % self.nsb)
        return Buf(self.st.enter_context(self.nc.psum_tensor(name, list(shape), dt)), name)

    def dma_sem(self, name):
        if name not in self.dsem_h:
            self.dsem_h[name] = self.st.enter_context(self.nc.semaphore('d_' + name))
        if name not in self.dcnt:
            self.dcnt[name] = 0
        return self.dsem_h[name]

    def _wait(self, eng, evs):
        w = self.waited[eng]
        need = {}
        for ev in evs:
            if ev is None:
                continue
            s, v, src = ev
            if src == eng and eng not in self.same_sync:
                continue
            k = id(s)
            if w.get(k, 0) >= v:
                continue
            if k not in need or need[k][1] < v:
                need[k] = (s, v)
        for k, (s, v) in need.items():
            self.prog[eng].append(('wait', s, v))
            w[k] = v

    def _deps(self, reads, writes):
        evs = []
        for b in reads:
            evs.append(b.lw)
        for b in writes:
            evs.append(b.lw)
            evs.extend(b.rd.values())
        return evs

    def _mark(self, ev, reads, writes):
        k = id(ev[0])
        for b in reads:
            if k not in b.rd or b.rd[k][1] < ev[1]:
                b.rd[k] = ev
        for b in writes:
            b.lw = ev
            b.rd = {}

    def op(self, eng, meth, kw, reads=(), writes=(), signal=True):
        if self.dry:
            return
        self._wait(eng, self._deps(reads, writes))
        ev = (self.sem[eng], self.cnt[eng] + 1, eng)
        if signal:
            self.cnt[eng] += 1
            self.pend[eng] = False
        else:
            self.pend[eng] = True
        self.prog[eng].append(('inst', meth, kw, self.sem[eng] if signal else None, 1))
        self._mark(ev, reads, writes)

    def dma(self, eng, semname, out, in_, reads=(), writes=()):
        if self.dry:
            return
        self._wait(eng, self._deps(reads, writes))
        s = self.dma_sem(semname)
        self.dcnt[semname] += 16
        ev = (s, self.dcnt[semname], 'dma')
        self.prog[eng].append(('inst', 'dma_start', dict(out=out, in_=in_), s, 16))
        self._mark(ev, reads, writes)

    def dma_mark(self, eng, semname, out, in_, reads, markbuf):
        if self.dry:
            return
        self._wait(eng, self._deps(reads, []))
        s = self.dma_sem(semname)
        self.dcnt[semname] += 16
        ev = (s, self.dcnt[semname], 'dma')
        self.prog[eng].append(('inst', 'dma_start', dict(out=out, in_=in_), s, 16))
        self._mark(ev, reads, [])
        markbuf.lw = ev
        markbuf.rd = {}

    def dma_nw(self, eng, semname, out, in_, buf):
        if self.dry:
            return
        s = self.dma_sem(semname)
        self.dcnt[semname] += 16
        self.prog[eng].append(('inst', 'dma_start', dict(out=out, in_=in_), s, 16))
        buf.lw = (s, self.dcnt[semname], 'dma')
        buf.rd = {}

    def barrier(self):
        if self.dry:
            return
        for e in self.ENG:
            assert not self.pend[e]
        evs = [(self.sem[f], self.cnt[f], f) for f in self.ENG]
        evs += [(self.dsem_h[n], c, 'dma') for n, c in self.dcnt.items()]
        for e in self.ENG:
            self._wait(e, [ev for ev in evs if ev[2] != e and ev[1] > 0])

    def final_wait(self, eng, semname):
        self.prog[eng].append(('wait', self.dsem_h[semname], self.dcnt[semname]))

    def emit(self):
        nc = self.nc
        block = self.st.enter_context(nc.Block())

        def replay(name, eng):
            for it in self.prog[name]:
                if it[0] == 'wait':
                    eng.wait_ge(it[1], it[2])
                else:
                    ins = getattr(eng, it[1])(**it[2])
                    if it[3] is not None:
                        ins.then_inc(it[3], it[4])

        @block.tensor
        def _(e):
            replay('pe', nc.tensor)

        @block.scalar
        def _(e):
            replay('act', nc.scalar)

        @block.vector
        def _(e):
            replay('dve', nc.vector)

        @block.gpsimd
        def _(e):
            replay('pool', nc.gpsimd)

        @block.sync
        def _(e):
            replay('sp', nc.sync)


class WPool:
    def __init__(self, S, n, shape, dt, name, eng):
        self.S = S
        self.n = n
        self.name = name
        self.eng = eng
        self.bufs = [S.sb(shape, dt, '%s_%d' % (name, i)) for i in range(n)]
        self.seq = []
        self.pos = 0
        self.issued = 0

    def start_real(self):
        self.pos = 0
        self.issued = 0
        for b in self.bufs:
            b.lw = None
            b.rd = {}

    def get(self, parts, reads=()):
        S = self.S
        if S.dry:
            self.seq.append((parts, reads))
            self.pos += 1
            return self.bufs[(self.pos - 1) % self.n]
        idx = self.pos
        lim = min(len(self.seq), idx + self.n)
        while self.issued < lim:
            j = self.issued
            b = self.bufs[j % self.n]
            p, r = self.seq[j]
            if any(rb.lw is None for rb in r):
                assert j > idx, 'weight source not ready'
                break
            for (dstf, src) in p:
                S.dma(self.eng, '%s_%d' % (self.name, j % self.n), dstf(b), src, reads=list(r), writes=[b])
            self.issued += 1
        self.pos += 1
        return self.bufs[idx % self.n]


def build(nc, NG, NPG, dbg=False):
    NTOK = NG * G
    NPRE = NPG * G

    def dram(name, shape, dt=F32, kind="ExternalInput"):
        return nc.dram_tensor(name, list(shape), dt, kind=kind).ap()

    xm = dram("xm", [NTOK, D])
    xp = dram("xp", [NPRE, D])
    memd = dram("mem", [256, D])
    class DSrc:
        def __init__(self, name, shape, perblock=False):
            self.f = dram(name, shape)
            self.b = dram(name + "_b", shape, BF16, kind="Internal")
            self.buf = Buf(self.b, name)
            self.perblock = perblock
            self.bufs = [Buf(self.b, '%s_%d' % (name, i)) for i in range(shape[0])] if perblock else None

        def __getitem__(self, i):
            if self.perblock:
                return (self.b[i], self.bufs[i])
            return (self.b[i], self.buf)

    w1 = DSrc("w1", [NB1, 128, KC, 128])
    w2 = DSrc("w2", [NB2, 128, KC, 256], perblock=True)
    wal = dram("wal", [128, KC, 16])
    wco = DSrc("wco", [16, 128, 16, 128])
    wgo = DSrc("wgo", [16, 128, 16, 128])
    wmo = DSrc("wmo", [16, 128, 4, 128])
    wout = DSrc("wout", [8, 128, 16, 256])
    wmk = DSrc("wmk", [4, 128, KC, 128])
    wmv = DSrc("wmv", [2, 128, KC, 256])
    DSRCS = [w2, w1, wco, wgo, wmo, wout, wmk, wmv]
    cf = dram("cf", [128, 256])
    cbd = dram("cb", [128, 512])
    vecd = dram("vec", [128, NV])
    dwwd = dram("dww", [128, 16, 31])
    wa2d = dram("wa2", [33, 1024])
    bcod = DSrc("bco", [16, 1, 128])
    DSRCS.append(bcod)
    fgd = dram("fg", [D])
    outd = dram("out", [NTOK, D], kind="ExternalOutput")
    dgs = dram("dgs", [16, 2, 128, 16, 128], BF16, kind="Internal")
    if dbg:
        dbgd = dram("dbg", [16, 128, 512], kind="ExternalOutput")

    st = contextlib.ExitStack()
    S = Sched(nc, st)
    CF = S.sb([128, 256], F32, 'CF')
    CB = S.sb([128, 512], BF16, 'CB')
    VEC = S.sb([128, NV], F32, 'VEC')
    HB = S.sb([128, 48], F32, 'HB')
    HW = S.sb([128, 16, 31], F32, 'HW')
    FG = S.sb([128, D], F32, 'FG')
    WA2 = S.sb([33, 1024], BF16, 'WA2')
    ONES1 = S.sb([1, 512], BF16, 'ONES1')
    WAL = S.sb([128, KC, 16], BF16, 'WAL')
    HT = S.sb([128, KC, G], BF16, 'HT')
    HTH = S.sb([128, KC, 32], BF16, 'HTH')
    XB = [S.sb([128, D], F32, 'XB%d' % i) for i in range(2)]
    S32 = [S.sb([128, 512], F32, 'S32_%d' % i) for i in range(8)]
    SBF = [S.sb([128, 512], BF16, 'SBF_%d' % i) for i in range(8)]
    UH = S.sb([128, 16, 32], BF16, 'UH')
    MKT = S.sb([128, 4, 256], BF16, 'MKT')
    MV = S.sb([128, 2, 512], BF16, 'MV')
    ALT = S.sb([33, G], BF16, 'ALT')
    AL = S.sb([128, NT, 8], F32, 'AL')
    STT = [S.sb([128, 8], F32, 'ST%d' % i) for i in range(4)]
    DGD = Buf(dgs, 'dgs')
    W1 = WPool(S, 4, [128, 17, 128], BF16, 'W1', 'sp')
    W2 = WPool(S, 3, [128, KC, 256], BF16, 'W2', 'sp')
    DG = WPool(S, 2, [128, 16, 128], BF16, 'DG', 'sp')
    ARENA_BYTES = 76 * 1024
    arena = st.enter_context(nc.sbuf_tensor('arena', [128, ARENA_BYTES], U8))
    PSB = [S.ps([128, 512], F32, 'PS%d' % i) for i in range(8)]

    ident_f = CF[:, 0:128]
    mask_f = CF[:, 128:256]
    ident_b = CB[:, 0:128]
    tri_inc = CB[:, 128:256]
    tri_dec = CB[:, 256:384]
    ones_b = CB[:, 384:512]
    negcol = CB[:, 255:256]

    state = {'bank': 0, 'st': 0, 'apos': 0, 'evac': 0}

    def bank():
        b = PSB[state['bank'] % 6]
        state['bank'] += 1
        return b

    def stt_buf():
        b = STT[state['st'] % 4]
        state['st'] += 1
        return b

    def arena_reset(off=0):
        state['apos'] = off

    def carve(free_shape, dt, name='a'):
        esz = 4 if dt == F32 else 2
        n = int(np.prod(free_shape))
        nbytes = (n * esz + 31) // 32 * 32
        off = state['apos']
        assert off + nbytes <= ARENA_BYTES, (name, off, nbytes)
        state['apos'] = off + nbytes
        v = arena[:, off:off + n * esz].bitcast(dt)
        if len(free_shape) == 2:
            v = v.rearrange("p (a b) -> p a b", a=free_shape[0])
        elif len(free_shape) == 3:
            v = v.rearrange("p (a b c) -> p a b c", a=free_shape[0], b=free_shape[1])
        return Buf(v, name)

    def act(out, in_, func, R, Wr, **kw):
        S.op('act', 'activation', dict(out=out, in_=in_, func=func, **kw), R, Wr)

    def mm(out, lhsT, rhs, start, stop, R, Wr, sig=None):
        S.op('pe', 'matmul', dict(out=out, lhsT=lhsT, rhs=rhs, start=bool(start), stop=bool(stop)), R, Wr,
             signal=bool(stop) if sig is None else sig)

    def tr(out, in_, ident, R, Wr):
        S.op('pe', 'transpose', dict(out=out, in_=in_, identity=ident), R, Wr)

    def stt(out, in0, scalar, in1, op0, op1, R, Wr):
        S.op('dve', 'scalar_tensor_tensor', dict(out=out, in0=in0, scalar=scalar, in1=in1, op0=op0, op1=op1), R, Wr)

    def ts(out, in0, s1, op0, R, Wr, s2=None, op1=None, eng='dve'):
        kw = dict(out=out, in0=in0, scalar1=s1, scalar2=s2, op0=op0)
        if op1 is not None:
            kw['op1'] = op1
        S.op(eng, 'tensor_scalar', kw, R, Wr)

    def tt(out, in0, in1, op, R, Wr, eng='dve'):
        S.op(eng, 'tensor_tensor', dict(out=out, in0=in0, in1=in1, op=op), R, Wr)

    def cp(out, in_, R, Wr, eng='dve'):
        if eng == 'act':
            act(out, in_, AF.Copy, R, Wr)
        else:
            S.op(eng, 'tensor_copy', dict(out=out, in_=in_), R, Wr)

    def evac_scaled(out, in_, scal, R, Wr):
        state['evac'] += 1
        if state['evac'] % 2 == 0:
            act(out, in_, AF.Identity, R, Wr, scale=scal)
        else:
            ts(out, in_, scal, ALU.mult, R, Wr)

    def rstd_from_ss(stb, n):
        act(stb[:, 1:2], stb[:, 0:1], AF.Ln, [stb], [stb], scale=1.0 / n, bias=EPS)
        act(stb[:, 2:3], stb[:, 1:2], AF.Exp, [stb], [stb], scale=-0.5)

    def w1get(src, kcs=16, bias=None):
        parts = [((lambda b, kcs=kcs: b[:, 0:kcs, :]), src[0])]
        rds = [src[1]]
        if bias is not None:
            parts.append(((lambda b: b[0:1, 16, :]), bias[0]))
            rds.append(bias[1])
        return W1.get(parts, reads=rds)

    def w2get(src):
        return W2.get([((lambda b: b[:]), src[0])], reads=[src[1]])

    KVS = {'next': 0, 'n': 0}

    def pc_kv(upto):
        f2 = w2.f.rearrange("b p (h k) c -> (b p h) (k c)", h=2)
        b2 = w2.b.rearrange("b p (h k) c -> (b p h) (k c)", h=2)
        while KVS['next'] < min(upto, B2_Z):
            blk = KVS['next']
            KVS['next'] += 1
            for hh in range(2):
                a = blk * 256 + hh * 128
                k = KVS['n']
                KVS['n'] += 1
                bnc = W1.bufs[k % 4]
                flat = bnc[:, 0:16, :].rearrange("p k c -> p (k c)")
                S.dma('pool', 'pck%d' % (k % 4), flat, f2[a:a + 128], writes=[bnc])
                S.dma_mark('sp', 'pcs_w2_%d' % blk, b2[a:a + 128], flat, [bnc], w2.bufs[blk])

    def precast(ds, blk0=None, blk1=None):
        shp = ds.f.shape
        if len(shp) == 4:
            f2 = ds.f.rearrange("b p k c -> (b p) (k c)")
            b2 = ds.b.rearrange("b p k c -> (b p) (k c)")
        else:
            f2 = ds.f.rearrange("b p c -> (b p) c")
            b2 = ds.b.rearrange("b p c -> (b p) c")
        rpb = shp[1]
        if ds.perblock:
            for i in range(blk0, blk1):
                S.dma_nw('pool', 'pc_%s_%d' % (ds.buf.name, i), b2[i * rpb:(i + 1) * rpb], f2[i * rpb:(i + 1) * rpb], ds.bufs[i])
            return
        R = f2.shape[0]
        rows_per = 64 if f2.shape[1] >= 2048 else 256
        for a in range(0, R, rows_per):
            e = min(R, a + rows_per)
            S.dma_nw('pool', 'pc_%s' % ds.buf.name, b2[a:e], f2[a:e], ds.buf)

    def dump(i, buf, ap):
        if dbg:
            S.dma('pool', 'dbg', dbgd[i], ap, reads=[buf])

    def load_norm(src, row0, ntiles, gcol, dstbuf, dst_fn, junk):
        prep_tile(src, row0, 0, gcol, dstbuf, dst_fn, junk, 'a')
        for i in range(ntiles):
            if i + 1 < ntiles:
                prep_tile(src, row0, i + 1, gcol, dstbuf, dst_fn, junk, 'a')
            prep_tile(src, row0, i, gcol, dstbuf, dst_fn, junk, 'b')

    def prep_tile(src, row0, i, gcol, dstbuf, dst_fn, junk, part='ab'):
        xb = XB[i % 2]
        xs = junk.xs[i % 2]
        if 'a' in part:
            S.dma('sp', 'xb%d' % (i % 2), xb[:], src[row0 + i * 128: row0 + (i + 1) * 128, :], writes=[xb])
            sb_ = stt_buf()
            act(junk[:], xb[:], AF.Square, [xb], [junk, sb_], accum_out=sb_[:, 0:1])
            rstd_from_ss(sb_, D)
            ts(xs[:], xb[:], sb_[:, 2:3], ALU.mult, [xb, sb_], [xs])
        if 'b' in part:
            for q in range(2):
                pb = bank()
                pb16 = pb[:].bitcast(BF16)
                for j in range(8):
                    kc = q * 8 + j
                    tr(pb16[:, j * 128:(j + 1) * 128], xs[:, kc * 128:(kc + 1) * 128], ident_b, [xs, CB], [pb])
                for j in range(8):
                    kc = q * 8 + j
                    evac_scaled(dst_fn(kc, i), pb16[:, j * 128:(j + 1) * 128], VEC[:, gcol + kc:gcol + kc + 1], [pb, VEC], [dstbuf])

    def carve_junk(name='junk'):
        j = carve([D], BF16, name)
        j.xs = [carve([D], BF16, name + '_xs%d' % i) for i in range(2)]
        return j

    def alpha_proj(HT=HT):
        pb = bank()
        for kc in range(KC):
            mm(pb[0:16, :], WAL[:, kc, :], HT[:, kc, :], kc == 0, kc == KC - 1, [WAL, HT], [pb])
        cp(ALT[0:16, :], pb[0:16, :], [pb], [ALT], eng='act')

    def decay_tile(i, c0, ncol, E, L, E3, e3_fn):
        decay_half_A(i, c0, ncol, E, L)
        decay_half_B(i, c0, ncol, L, E3, e3_fn)

    def decay_half_A(i, c0, ncol, E, L):
        for c in range(0, ncol, 512):
            w = min(512, ncol - c)
            pb = bank()
            mm(pb[:, 0:w], ALT[0:33, i * 128:(i + 1) * 128], WA2[0:33, c0 + c:c0 + c + w], True, True, [ALT, WA2], [pb])
            act(E[:, c:c + w], pb[:, 0:w], AF.Exp, [pb], [E], scale=-1.0)
        act(L[:, 0:ncol], E[:, 0:ncol], AF.Ln, [E], [L], bias=1.0)

    def decay_half_B(i, c0, ncol, L, E3, e3_fn):
        for c in range(0, ncol, 512):
            w = min(512, ncol - c)
            pb = bank()
            mm(pb[:, 0:w], tri_dec, L[:, c:c + w], True, True, [CB, L], [pb])
            act(e3_fn(c, w), pb[:, 0:w], AF.Exp, [pb], [E3])
        pb = bank()
        nkb = ncol // 128
        for kb in range(nkb):
            mm(pb[:, kb:kb + 1], L[:, kb * 128:(kb + 1) * 128], negcol, True, True, [L, CB], [pb])
        kb0 = c0 // 128
        act(AL[:, i, kb0:kb0 + nkb], pb[:, 0:nkb], AF.Exp, [pb], [AL])

    def state_update(i, h, KDEC, kdec_fn, VT, vt_ap, to_bf):
        for kb2 in range(2):
            kb = 2 * h + kb2
            pb = bank()
            mm(pb[:], kdec_fn(i, kb2), vt_ap, True, True, [KDEC, VT], [pb])
            stt(S32[kb][:], S32[kb][:], AL[:, i, kb:kb + 1], pb[:], ALU.mult, ALU.add, [S32[kb], AL, pb], [S32[kb]])
            if to_bf:
                cp(SBF[kb][:], S32[kb][:], [S32[kb]], [SBF[kb]], eng='act')

    def prefix_bufs():
        arena_reset(0)
        junk = carve_junk('junk')
        E = carve([1024], F32, 'E')
        Ls = [carve([1024], BF16, 'L%d' % i) for i in range(2)]
        E3 = carve([NT, 1024], F32, 'E3')
        KDEC = carve([NT, 1024], BF16, 'KDEC')
        VT = carve([NT, D], BF16, 'VT')
        HT2 = carve([KC, G], BF16, 'HT2')
        return junk, E, Ls, E3, KDEC, VT, HT2

    def prefix_group(g, last, pbufs, HT_main=HT):
        junk, E, Ls, E3, KDEC, VT, HT2 = pbufs
        mark('p%d' % g)
        HT = HT_main if g % 2 == 0 else HT2
        HTn = HT2 if g % 2 == 0 else HT_main

        def prep_next(i, part='ab'):
            if not last:
                prep_tile(xp, (g + 1) * G, i, V_ING, HTn, lambda kc, i: HTn[:, kc, i * 128:(i + 1) * 128], junk, part)

        if g == 0:
            load_norm(xp, 0, NT, V_ING, HT, lambda kc, i: HT[:, kc, i * 128:(i + 1) * 128], junk)
        if last:
            cp(HTH[:], HT[:, :, G - 32:G], [HT], [HTH], eng='dve')
        alpha_proj(HT)

        def k_block(nb):
            w = w2get(w2[B2_K + nb])
            for i in range(NT):
                pb = bank()
                for kc in range(KC):
                    mm(pb[:, 0:256], HT[:, kc, i * 128:(i + 1) * 128], w[:, kc, :], kc == 0, kc == KC - 1, [HT, w], [pb])
                tt(KDEC[:, i, nb * 256:(nb + 1) * 256], pb[:, 0:256], E3[:, i, nb * 256:(nb + 1) * 256], ALU.mult, [pb, E3], [KDEC])

        def v_block(nb):
            w = w2get(w2[B2_V + nb])
            for i in range(NT):
                pb = bank()
                for kc in range(KC):
                    mm(pb[:, 0:256], HT[:, kc, i * 128:(i + 1) * 128], w[:, kc, :], kc == 0, kc == KC - 1, [HT, w], [pb])
                cp(VT[:, i, nb * 256:(nb + 1) * 256], pb[:, 0:256], [pb], [VT], eng=('act' if (nb + i) % 2 else 'dve'))

        if g == 0:
            for i in range(NT):
                decay_tile(i, 0, 1024, E, Ls[i % 2], E3, lambda c, w, i=i: E3[:, i, c:c + w])
            for nb in range(4):
                if nb == 1:
                    prep_next(0, 'a')
                if nb == 2:
                    prep_next(0, 'b')
                pc_kv(B2_K + nb + 3)
                k_block(nb)
            for nb in range(8):
                if nb in (0, 3, 6):
                    prep_next({0: 1, 3: 2, 6: 3}[nb], 'a')
                if nb in (1, 4, 7):
                    prep_next({1: 1, 4: 2, 7: 3}[nb], 'b')
                pc_kv(B2_V + nb + 3)
                v_block(nb)
        else:
            for nb in range(8):
                if nb < NT:
                    decay_half_A(nb, 0, 1024, E, Ls[nb % 2])
                if nb in (1, 4, 7):
                    prep_next({1: 0, 4: 1, 7: 2}[nb], 'a')
                if nb in (2, 5):
                    prep_next({2: 0, 5: 1}[nb], 'b')
                v_block(nb)
                if nb < NT:
                    decay_half_B(nb, 0, 1024, Ls[nb % 2], E3, lambda c, w, i=nb: E3[:, i, c:c + w])
            for nb in range(4):
                if nb == 1:
                    prep_next(3, 'a')
                if nb in (0, 2):
                    prep_next({0: 2, 2: 3}[nb], 'b')
                k_block(nb)
        if g == 0:
            pc_kv(B2_Z)
            background_precast()
        for i in range(NT):
            for h in range(4):
                state_update(i, h, KDEC, lambda i, kb2, h=h: KDEC[:, i, (2 * h + kb2) * 128:(2 * h + kb2 + 1) * 128],
                             VT, VT[:, i, h * 512:(h + 1) * 512], False)

    def startup():
        S.dma('sp', 'c0a', CF[:], cf[:, :], writes=[CF])
        S.dma('sp', 'c0b', VEC[:], vecd[:, :], writes=[VEC])
        S.dma('sp', 'c0c', HW[:], dwwd[:, :, :], writes=[HW])
        S.dma('sp', 'c0d', FG[:], fgd.partition_broadcast(128), writes=[FG])
        S.dma('pool', 'c1a', CB[:], cbd[:, :], writes=[CB])
        S.dma('pool', 'c1b', WA2[:], wa2d[:, :], writes=[WA2])
        S.dma('pool', 'c1c', WAL[:], wal[:, :, :], writes=[WAL])
        precast(bcod)

    def background_precast():
        precast(wmk)
        precast(wmv)
        precast(w1)
        precast(w2, B2_Z, NB2)
        for ds in (wco, wgo, wmo, wout):
            precast(ds)
        S.op('dve', 'memset', dict(ap=ONES1[:], constant=1.0), [], [ONES1])
        S.op('dve', 'memset', dict(ap=ALT[:], constant=0.0), [], [ALT])
        S.op('dve', 'memset', dict(ap=ALT[32:33, :], constant=1.0), [], [ALT])
        for kb in range(8):
            S.op('dve', 'memset', dict(ap=S32[kb][:], constant=0.0), [], [S32[kb]])
        ts(HB[:], VEC[:, V_BG:V_BG + 48], 0.5, ALU.mult, [VEC], [HB])
        ts(HW[:], HW[:], 0.5, ALU.mult, [HW], [HW])

    def dg_build():
        for cbk in range(16):
            for half in range(2):
                b = DG.bufs[(cbk * 2 + half) % 2]
                ntap = 16 if half == 0 else 15
                for jj in range(ntap):
                    j = half * 16 + jj
                    ts(b[:, jj, :], ident_f, HW[:, cbk, j:j + 1], ALU.mult, [CF, HW], [b])
                if half == 1:
                    S.op('dve', 'memset', dict(ap=b[:, 15, :], constant=0.0), [], [b])
                S.dma('sp', 'dgw', dgs[cbk, half], b[:], reads=[b], writes=[DGD])

    def mem_kv():
        arena_reset(0)
        junk = carve_junk('junk')
        MT = carve([KC, 256], BF16, 'MT')
        load_norm(memd, 0, 2, V_MLG, MT, lambda kc, i: MT[:, kc, i * 128:(i + 1) * 128], junk)
        for h in range(4):
            w = w1get(wmk[h])
            pb = bank()
            for kc in range(KC):
                mm(pb[:, 0:256], w[:, kc, :], MT[:, kc, :], kc == 0, kc == KC - 1, [w, MT], [pb])
            cp(MKT[:, h, :], pb[:, 0:256], [pb], [MKT], eng='act')
        for nb in range(2):
            w = w2get(wmv[nb])
            for mb in range(2):
                pb = bank()
                for kc in range(KC):
                    mm(pb[:, 0:256], MT[:, kc, mb * 128:(mb + 1) * 128], w[:, kc, :], kc == 0, kc == KC - 1, [MT, w], [pb])
                cp(MV[:, mb, nb * 256:(nb + 1) * 256], pb[:, 0:256], [pb], [MV], eng='dve')
        S.barrier()

    def halo_stage():
        for kb in range(8):
            cp(SBF[kb][:], S32[kb][:], [S32[kb]], [SBF[kb]], eng='act')
        S.barrier()

    def main_group(g):
        arena_reset(0)
        UF = carve([16, G], BF16, 'UF')
        OZ = carve([16, G], BF16, 'OZ')
        OMZ = carve([4, G], BF16, 'OMZ')
        base = state['apos']

        mark('g%d_s0' % g)
        junk = carve_junk('junk')
        load_norm(xm, g * G, NT, V_ING, HT, lambda kc, i: HT[:, kc, i * 128:(i + 1) * 128], junk)
        dump(0, HT, HT[:, 0, :])
        S.barrier()

        mark('g%d_s1' % g)
        arena_reset(base)
        YC = carve([16, G], BF16, 'YC')
        Us = [carve([32 + G], BF16, 'U%d' % i) for i in range(2)]
        YSQs = [carve([G], BF16, 'YSQ%d' % i) for i in range(2)]
        THBs = [carve([G], F32, 'THB%d' % i) for i in range(2)]
        T1s = [carve([G], F32, 'T1_%d' % i) for i in range(2)]
        T2s = [carve([G], F32, 'T2_%d' % i) for i in range(2)]
        MU = carve([G], F32, 'MU')
        MSQ = carve([G], F32, 'MSQ')
        P6, P7 = PSB[6], PSB[7]
        THBh = [carve([32], F32, 'THBh%d' % i) for i in range(2)]
        for cbk in range(16):
            wa = w1get(w1[B1_A + cbk])
            pa = bank()
            for kc in range(KC):
                mm(pa[:], wa[:, kc, :], HT[:, kc, :], kc == 0, kc == KC - 1, [wa, HT], [pa])
            if g == 0:
                ph = bank()
                for kc in range(KC):
                    mm(ph[:, 0:32], wa[:, kc, :], HTH[:, kc, :], kc == 0, kc == KC - 1, [wa, HTH], [ph])
            wb = w1get(w1[B1_B + cbk])
            pb = bank()
            for kc in range(KC):
                mm(pb[:], wb[:, kc, :], HT[:, kc, :], kc == 0, kc == KC - 1, [wb, HT], [pb])
            if g == 0:
                for kc in range(KC):
                    mm(ph[:, 32:64], wb[:, kc, :], HTH[:, kc, :], kc == 0, kc == KC - 1, [wb, HTH], [ph])
                thh = THBh[cbk % 2]
                act(thh[:], ph[:, 32:64], AF.Tanh, [ph], [thh], scale=0.5)
                stt(UH[:, cbk, :], thh[:], 1.0, ph[:, 0:32], ALU.add, ALU.mult, [thh, ph], [UH])
            wz = w1get(w1[B1_Z + cbk])
            pz = bank()
            for kc in range(KC):
                mm(pz[:], wz[:, kc, :], HT[:, kc, :], kc == 0, kc == KC - 1, [wz, HT], [pz])
            thb = THBs[cbk % 2]
            U = Us[cbk % 2]
            act(thb[:], pb[:], AF.Tanh, [pb], [thb], scale=0.5)
            cp(U[:, 0:32], UH[:, cbk, :], [UH], [U], eng='dve')
            stt(U[:, 32:32 + G], thb[:], 1.0, pa[:], ALU.add, ALU.mult, [thb, pa], [U])
            cp(UH[:, cbk, :], U[:, G:G + 32], [U], [UH], eng='dve')
            act(UF[:, cbk, :], pz[:], AF.Silu, [pz], [UF])
            pc = bank()
            dgx = DG.get([((lambda b: b[:]), dgs[cbk, 0])], reads=[DGD])
            for j in range(16):
                mm(pc[:], dgx[:, j, :], U[:, 2 + j:2 + j + G], j == 0, False, [dgx, U], [pc])
            dgx = DG.get([((lambda b: b[:]), dgs[cbk, 1])], reads=[DGD])
            for j in range(16, 31):
                mm(pc[:], dgx[:, j - 16, :], U[:, 2 + j:2 + j + G], False, j == 30, [dgx, U], [pc])
            ysq = YSQs[cbk % 2]
            act(YC[:, cbk, :], pc[:], AF.Identity, [pc, VEC], [YC], bias=VEC[:, V_DWB + cbk:V_DWB + cbk + 1])
            act(ysq[:], pc[:], AF.Square, [pc, VEC], [ysq], bias=VEC[:, V_DWB + cbk:V_DWB + cbk + 1])
            mm(P6[:], ones_b, YC[:, cbk, :], cbk == 0, cbk == 15, [CB, YC], [P6], sig=True)
            mm(P7[:], ones_b, ysq[:], cbk == 0, cbk == 15, [CB, ysq], [P7], sig=True)
        mark('g%d_s1ln' % g)
        act(MU[:], P6[:], AF.Copy, [P6], [MU], scale=1.0 / D)
        tt(MSQ[:], MU[:], MU[:], ALU.mult, [MU], [MSQ])
        stt(MSQ[:], P7[:], 1.0 / D, MSQ[:], ALU.mult, ALU.subtract, [P7, MSQ], [MSQ])
        ts(MSQ[:], MSQ[:], 0.0, ALU.max, [MSQ], [MSQ])
        act(MSQ[:], MSQ[:], AF.Ln, [MSQ], [MSQ], bias=EPS)
        act(P6[:], MSQ[:], AF.Exp, [MSQ], [P6], scale=-0.5)
        stt(P7[:], MU[:], -1.0, P6[:], ALU.mult, ALU.mult, [MU, P6], [P7])
        for cbk in range(16):
            t1 = T1s[cbk % 2]
            t2 = T2s[cbk % 2]
            tt(t1[:], YC[:, cbk, :], P6[:], ALU.mult, [YC, P6], [t1])
            tt(t1[:], t1[:], P7[:], ALU.add, [t1, P7], [t1])
            act(t2[:], t1[:], AF.Silu, [t1, VEC], [t2], scale=VEC[:, V_LNG + cbk:V_LNG + cbk + 1], bias=VEC[:, V_LNB + cbk:V_LNB + cbk + 1])
            tt(UF[:, cbk, :], t2[:], UF[:, cbk, :], ALU.mult, [t2, UF], [UF], eng='pool')
        dump(1, UF, UF[:, 0, :])
        dump(5, YC, YC[:, 0, :])
        S.barrier()

        mark('g%d_s3' % g)
        arena_reset(base)
        junk = carve([512], BF16, 'junk')
        E = carve([256], F32, 'E')
        Lh = carve([NT, 256], BF16, 'Lh')
        E3 = carve([NT, 256], F32, 'E3')
        E1s = [carve([G], F32, 'E1_%d' % i) for i in range(2)]
        E2s = [carve([G], F32, 'E2_%d' % i) for i in range(2)]
        QE = carve([2, G], BF16, 'QE')
        KD = carve([2, G], BF16, 'KD')
        KDEC = carve([NT, 256], BF16, 'KDEC')
        VT = carve([NT, 512], BF16, 'VT')
        ZT = carve([NT, 512], BF16, 'ZT')
        PTs = [carve([128], BF16, 'PT%d' % i) for i in range(2)]
        OTs = [carve([512], F32, 'OT%d' % i) for i in range(2)]
        alpha_proj()
        for h in range(4):
            for i in range(NT):
                _decay_tile_h(i, h, E, Lh, E3)
            for kb2 in range(2):
                pbf = bank()
                for i in range(NT):
                    mm(pbf[:, i * 128:(i + 1) * 128], Lh[:, i, kb2 * 128:(kb2 + 1) * 128], tri_inc, True, True, [Lh, CB], [pbf])
                e1 = E1s[kb2]
                e2 = E2s[kb2]
                act(e1[:], pbf[:], AF.Exp, [pbf, LN16], [e1], bias=LN16[:, 0:1])
                act(e2[:], pbf[:], AF.Exp, [pbf], [e2], scale=-1.0)
                wq = w1get(w1[B1_Q + 2 * h + kb2])
                pq = bank()
                for kc in range(KC):
                    mm(pq[:], wq[:, kc, :], HT[:, kc, :], kc == 0, kc == KC - 1, [wq, HT], [pq])
                tt(QE[:, kb2, :], pq[:], e1[:], ALU.mult, [pq, e1], [QE])
                wk = w1get(w1[B1_K + 2 * h + kb2])
                pk = bank()
                for kc in range(KC):
                    mm(pk[:], wk[:, kc, :], HT[:, kc, :], kc == 0, kc == KC - 1, [wk, HT], [pk])
                tt(KD[:, kb2, :], pk[:], e2[:], ALU.mult, [pk, e2], [KD])
            w = w2get(w2[B2_K + h])
            for i in range(NT):
                pb = bank()
                for kc in range(KC):
                    mm(pb[:, 0:256], HT[:, kc, i * 128:(i + 1) * 128], w[:, kc, :], kc == 0, kc == KC - 1, [HT, w], [pb])
                tt(KDEC[:, i, :], pb[:, 0:256], E3[:, i, :], ALU.mult, [pb, E3], [KDEC])
            for nb2 in range(2):
                w = w2get(w2[B2_V + 2 * h + nb2])
                for i in range(NT):
                    pb = bank()
                    for kc in range(KC):
                        mm(pb[:, 0:256], HT[:, kc, i * 128:(i + 1) * 128], w[:, kc, :], kc == 0, kc == KC - 1, [HT, w], [pb])
                    cp(VT[:, i, nb2 * 256:(nb2 + 1) * 256], pb[:, 0:256], [pb], [VT], eng=('act' if i % 2 else 'dve'))
            for nb2 in range(2):
                w = w2get(w2[B2_Z + 2 * h + nb2])
                for i in range(NT):
                    pb = bank()
                    for kc in range(KC):
                        mm(pb[:, 0:256], HT[:, kc, i * 128:(i + 1) * 128], w[:, kc, :], kc == 0, kc == KC - 1, [HT, w], [pb])
                    act(ZT[:, i, nb2 * 256:(nb2 + 1) * 256], pb[:, 0:256], AF.Silu, [pb], [ZT])
            for i in range(NT):
                tsl = slice(i * 128, (i + 1) * 128)
                psc = bank()
                for kb2 in range(2):
                    mm(psc[:, 0:128], KD[:, kb2, tsl], QE[:, kb2, tsl], kb2 == 0, kb2 == 1, [KD, QE], [psc])
                pt = PTs[i % 2]
                tt(pt[:], psc[:, 0:128], mask_f, ALU.mult, [psc, CF], [pt])
                po = bank()
                mm(po[:], QE[:, 0, tsl], SBF[2 * h][:], True, False, [QE, SBF[2 * h]], [po])
                mm(po[:], QE[:, 1, tsl], SBF[2 * h + 1][:], False, False, [QE, SBF[2 * h + 1]], [po])
                mm(po[:], pt[:], VT[:, i, :], False, True, [pt, VT], [po])
                state_update(i, h, KDEC, lambda i, kb2: KDEC[:, i, kb2 * 128:(kb2 + 1) * 128], VT, VT[:, i, :], True)
                sb_ = stt_buf()
                act(junk[:], po[:], AF.Square, [po], [junk, sb_], accum_out=sb_[:, 0:1])
                rstd_from_ss(sb_, 512)
                ot = OTs[i % 2]
                stt(ot[:], po[:], sb_[:, 2:3], ZT[:, i, :], ALU.mult, ALU.mult, [po, sb_, ZT], [ot])
                ptb = bank()
                for vb in range(4):
                    tr(ptb[:, vb * 128:(vb + 1) * 128], ot[:, vb * 128:(vb + 1) * 128], ident_f, [ot, CF], [ptb])
                for vb in range(4):
                    c = 4 * h + vb
                    evac_scaled(OZ[:, c, tsl], ptb[:, vb * 128:(vb + 1) * 128], VEC[:, V_GNG + c:V_GNG + c + 1], [ptb, VEC], [OZ])
        dump(2, OZ, OZ[:, 0, :])
        dump(6, OZ, OZ[:, 4, :])
        S.barrier()

        mark('g%d_s4' % g)
        arena_reset(base)
        MQ = carve([4, G], BF16, 'MQ')
        SMZ = carve([4, G], BF16, 'SMZ')
        Ps = [carve([256], F32, 'P%d' % i) for i in range(2)]
        PNs = [carve([256], BF16, 'PN%d' % i) for i in range(2)]
        PTm = [carve([256], BF16, 'PTm%d' % i) for i in range(2)]
        for h in range(4):
            w = w1get(w1[B1_MQ + h])
            pb = bank()
            for kc in range(KC):
                mm(pb[:], w[:, kc, :], HT[:, kc, :], kc == 0, kc == KC - 1, [w, HT], [pb])
            act(MQ[:, h, :], pb[:], AF.Copy, [pb], [MQ], scale=float(128 ** -0.5))
            w = w1get(w1[B1_MZ + h])
            pb = bank()
            for kc in range(KC):
                mm(pb[:], w[:, kc, :], HT[:, kc, :], kc == 0, kc == KC - 1, [w, HT], [pb])
            act(SMZ[:, h, :], pb[:], AF.Silu, [pb], [SMZ])
        its = [(i, h) for i in range(NT) for h in range(4)]

        def st_A(n):
            i, h = its[n]
            tsl = slice(i * 128, (i + 1) * 128)
            psc = bank()
            mm(psc[:, 0:256], MQ[:, h, tsl], MKT[:, h, :], True, True, [MQ, MKT], [psc])
            sb_ = stt_buf()
            S.op('dve', 'tensor_reduce', dict(out=sb_[:, 0:1], in_=psc[:, 0:256], axis=AX.X, op=ALU.max), [psc], [sb_])
            ts(sb_[:, 1:2], sb_[:, 0:1], -1.0, ALU.mult, [sb_], [sb_])
            P = Ps[n % 2]
            PN = PNs[n % 2]
            act(P[:], psc[:, 0:256], AF.Exp, [psc, sb_], [P, sb_], bias=sb_[:, 1:2], accum_out=sb_[:, 2:3])
            S.op('dve', 'reciprocal', dict(out=sb_[:, 3:4], in_=sb_[:, 2:3]), [sb_], [sb_])
            ts(PN[:], P[:], sb_[:, 3:4], ALU.mult, [P, sb_], [PN])

        def st_B(n):
            PN = PNs[n % 2]
            PT = PTm[n % 2]
            ptb = bank()
            ptb16 = ptb[:].bitcast(BF16)
            for mb in range(2):
                tr(ptb16[:, mb * 128:(mb + 1) * 128], PN[:, mb * 128:(mb + 1) * 128], ident_b, [PN, CB], [ptb])
            cp(PT[:], ptb16[:, 0:256], [ptb], [PT], eng='act')

        def st_C(n):
            i, h = its[n]
            tsl = slice(i * 128, (i + 1) * 128)
            PT = PTm[n % 2]
            pom = bank()
            for mb in range(2):
                mm(pom[:, 0:128], MV[:, mb, h * 128:(h + 1) * 128], PT[:, mb * 128:(mb + 1) * 128], mb == 0, mb == 1, [MV, PT], [pom])
            tt(OMZ[:, h, tsl], pom[:, 0:128], SMZ[:, h, tsl], ALU.mult, [pom, SMZ], [OMZ])

        NI = len(its)
        for n in range(NI + 2):
            if n < NI:
                st_A(n)
            if n >= 2:
                st_C(n - 2)
            if 1 <= n <= NI:
                st_B(n - 1)
        dump(3, OMZ, OMZ[:, 0, :])
        S.barrier()

        mark('g%d_s5' % g)
        arena_reset(base)
        YS = carve([16, G], BF16, 'YS')
        THs = [carve([G], F32, 'TH%d' % i) for i in range(2)]
        THs.append(THs[0])
        Ms = [carve([G], F32, 'M%d' % i) for i in range(2)]
        Ms.append(Ms[1])
        for db in range(16):
            for br in range(3):
                if br == 0:
                    wy = w1get(wco[db], bias=bcod[db])
                    src, nk = UF, 16
                elif br == 1:
                    wy = w1get(wgo[db])
                    src, nk = OZ, 16
                else:
                    wy = w1get(wmo[db], kcs=4)
                    src, nk = OMZ, 4
                py = bank()
                for kc in range(nk):
                    mm(py[:], wy[:, kc, :], src[:, kc, :], kc == 0, (kc == nk - 1) and br != 0, [wy, src], [py])
                if br == 0:
                    mm(py[:], wy[0:1, 16, :], ONES1[0:1, :], False, True, [wy, ONES1], [py])
                wg = w1get(w1[B1_G + br * 16 + db])
                pg = bank()
                for kc in range(KC):
                    mm(pg[:], wg[:, kc, :], HT[:, kc, :], kc == 0, kc == KC - 1, [wg, HT], [pg])
                th = THs[br]
                act(th[:], pg[:], AF.Tanh, [pg, HB], [th], scale=0.5, bias=HB[:, br * 16 + db:br * 16 + db + 1])
                stt(Ms[br][:], th[:], 1.0, py[:], ALU.add, ALU.mult, [th, py], [Ms[br]])
                if br == 1:
                    tt(Ms[0][:], Ms[0][:], Ms[1][:], ALU.add, [Ms[0], Ms[1]], [Ms[0]])
            tt(YS[:, db, :], Ms[0][:], Ms[2][:], ALU.add, [Ms[0], Ms[2]], [YS])
        dump(4, YS, YS[:, 0, :])
        mark('g%d_s5out' % g)
        XBs = [XB[0], XB[1], carve([D], F32, 'XB2'), carve([D], F32, 'XB3')]
        for i in range(NT):
            S.dma('sp', 'xo%d' % i, XBs[i][:], xm[g * G + i * 128:g * G + (i + 1) * 128, :], writes=[XBs[i]])
        for nb in range(8):
            w = w2get(wout[nb])
            for i in range(NT):
                xb = XBs[i]
                pb = bank()
                for db in range(16):
                    mm(pb[:, 0:256], YS[:, db, i * 128:(i + 1) * 128], w[:, db, :], db == 0, db == 15, [YS, w], [pb])
                stt(xb[:, nb * 256:(nb + 1) * 256], pb[:, 0:256], 0.5, xb[:, nb * 256:(nb + 1) * 256], ALU.mult, ALU.add, [pb, xb], [xb])
        junkv = YS[:, 0:4, :]
        for i in range(NT):
            xb = XBs[i]
            sb_ = stt_buf()
            act(junkv, xb[:].rearrange("p (a b) -> p a b", a=4), AF.Square, [xb], [YS, sb_], accum_out=sb_[:, 0:1])
            rstd_from_ss(sb_, D)
            stt(xb[:], xb[:], sb_[:, 2:3], FG[:], ALU.mult, ALU.mult, [xb, sb_, FG], [xb])
            S.dma('sp', 'out', outd[g * G + i * 128:g * G + (i + 1) * 128, :], xb[:], reads=[xb])
        S.barrier()

    def _decay_tile_h(i, h, E, Lh, E3):
        c0 = h * 256
        pb = bank()
        mm(pb[:, 0:256], ALT[0:33, i * 128:(i + 1) * 128], WA2[0:33, c0:c0 + 256], True, True, [ALT, WA2], [pb])
        act(E[:], pb[:, 0:256], AF.Exp, [pb], [E], scale=-1.0)
        act(Lh[:, i, :], E[:], AF.Ln, [E], [Lh], bias=1.0)
        pb = bank()
        mm(pb[:, 0:256], tri_dec, Lh[:, i, :], True, True, [CB, Lh], [pb])
        act(E3[:, i, :], pb[:, 0:256], AF.Exp, [pb], [E3])
        pb = bank()
        for kb2 in range(2):
            mm(pb[:, kb2:kb2 + 1], Lh[:, i, kb2 * 128:(kb2 + 1) * 128], negcol, True, True, [Lh, CB], [pb])
        act(AL[:, i, 2 * h:2 * h + 2], pb[:, 0:2], AF.Exp, [pb], [AL])

    LN16 = S.sb([128, 1], F32, 'LN16')

    def mark(name):
        if not S.dry:
            MARKS.append((name, sum(1 for it in S.prog['pe'] if it[0] == 'inst')))

    def emit_all():
        S.op('dve', 'memset', dict(ap=LN16[:], constant=float(np.log(1.0 / 16.0))), [], [LN16])
        startup()
        pbufs = prefix_bufs()
        KVS['next'] = 0
        KVS['n'] = 0
        pc_kv(2)
        for g in range(NPG):
            prefix_group(g, g == NPG - 1, pbufs)
            if g == 0:
                dg_build()
        if NPG == 0:
            dg_build()
        S.barrier()
        mark('memkv')
        mem_kv()
        mark('halo')
        halo_stage()
        for g in range(NG):
            main_group(g)

    S.dry = True
    emit_all()
    S.dry = False
    S.reset()
    state.update({'bank': 0, 'st': 0, 'apos': 0, 'evac': 0})
    for p in (W1, W2, DG):
        p.start_real()
    allb = [CF, CB, VEC, HB, HW, FG, WA2, ONES1, WAL, HT, HTH, UH, MKT, MV, ALT, AL, DGD, LN16] + XB + S32 + SBF + STT + PSB
    for b in allb:
        b.lw = None
        b.rd = {}
    emit_all()
    mark('end')
    S.final_wait('sp', 'out')
    if dbg:
        S.final_wait('pool', 'dbg')
    S.emit()
    st.close()
    return nc


_CACHE = {}
MARKS = []


def _host_consts():
    ident = np.eye(128, dtype=np.float32)
    s = np.arange(128)[:, None]
    t = np.arange(128)[None, :]
    mask = (s <= t).astype(np.float32)
    tri_inc = np.where(s <= t, -1.0 / 16.0, 0.0).astype(np.float32)
    tri_dec = np.where(s > t, -1.0 / 16.0, 0.0).astype(np.float32)
    ones = np.ones((128, 128), np.float32)
    cf = np.concatenate([ident, mask], axis=1)
    cb = np.concatenate([ident, tri_inc, tri_dec, ones], axis=1)
    return np.ascontiguousarray(cf), np.ascontiguousarray(cb)


def _blk1(w, ncols_blk=128):
    K, N = w.shape
    return np.ascontiguousarray(w.reshape(K // 128, 128, N // ncols_blk, ncols_blk).transpose(2, 1, 0, 3))


def _pvec(v):
    return np.ascontiguousarray(v.reshape(-1, 128).T)


def prep_weights(inp):
    w_in = np.asarray(inp["w_in"])[0]
    cols1 = np.concatenate([w_in[:, C_A:C_A + 2048], w_in[:, C_B:C_B + 2048], w_in[:, C_Z:C_Z + 2048],
                            w_in[:, C_Q:C_Q + 1024], w_in[:, C_K:C_K + 1024], w_in[:, C_MQ:C_MQ + 512],
                            w_in[:, C_MZ:C_MZ + 512], w_in[:, C_GATE:C_GATE + 6144]], axis=1)
    cols2 = np.concatenate([w_in[:, C_K:C_K + 1024], w_in[:, C_V:C_V + 2048], w_in[:, C_GZ:C_GZ + 2048]], axis=1)
    cf, cb = _host_consts()
    vec = np.concatenate([_pvec(np.asarray(inp["b_gate"])[0]), _pvec(np.asarray(inp["dw_b"])[0]),
                          _pvec(np.asarray(inp["conv_ln_g"])[0]), _pvec(np.asarray(inp["conv_ln_b"])[0]),
                          _pvec(np.asarray(inp["ln_in_g"])[0]), _pvec(np.asarray(inp["gla_norm_g"])[0]),
                          _pvec(np.asarray(inp["mem_ln_g"])[0])], axis=1)
    dw = np.asarray(inp["dw_w"])[0]
    dww = np.ascontiguousarray(dw.reshape(31, 16, 128).transpose(2, 1, 0))
    wa2 = np.zeros((33, 1024), np.float32)
    wa2[0:16] = np.asarray(inp["w_alpha2"])[0]
    wa2[32] = np.asarray(inp["b_alpha"])[0]
    wkv = np.asarray(inp["w_mem_kv"])[0]
    d = {
        "w1": _blk1(cols1), "w2": _blk1(cols2, 256),
        "wal": np.ascontiguousarray(w_in[:, C_AL:C_AL + 16].reshape(16, 128, 16).transpose(1, 0, 2)),
        "wco": _blk1(np.asarray(inp["w_conv_out"])[0]), "wgo": _blk1(np.asarray(inp["w_gla_out"])[0]),
        "wmo": _blk1(np.asarray(inp["w_mem_out"])[0]), "wout": _blk1(np.asarray(inp["w_out"])[0], 256),
        "wmk": _blk1(wkv[:, 0:512]), "wmv": _blk1(wkv[:, 512:1024], 256),
        "cf": cf, "cb": cb, "vec": np.ascontiguousarray(vec.astype(np.float32)), "dww": dww.astype(np.float32),
        "wa2": wa2, "bco": np.ascontiguousarray(np.asarray(inp["b_conv_out"])[0].reshape(16, 1, 128)),
        "fg": np.ascontiguousarray(np.asarray(inp["final_g"]).astype(np.float32)),
    }
    return d


def run(inp, NG=NG_FULL, NPG=NPG_FULL, cores=N_CORES, dbg=False):
    key = (NG, NPG, dbg)
    nc = bass.Bass("TRN2", target_bir_lowering=False)
    build(nc, NG, NPG, dbg)
    wd = prep_weights(inp)
    x = np.asarray(inp["x"])
    mem = np.asarray(inp["mem"])
    SEG = x.shape[1] // 4
    in_maps = []
    for c in range(cores):
        b, j = c // 4, c % 4
        s0 = j * SEG
        xm = np.ascontiguousarray(x[b, s0:s0 + NG * G])
        npre = NPG * G
        xp = np.zeros((npre, D), np.float32)
        take = min(npre, s0)
        if take > 0:
            xp[npre - take:] = x[b, s0 - take:s0]
        m = dict(wd)
        m["xm"] = xm
        m["xp"] = xp
        m["mem"] = np.ascontiguousarray(mem[b])
        in_maps.append(m)
    res = run_bass_kernel_spmd(nc, in_maps, core_ids=list(range(cores)))
    return res


def kernel(**inputs):
    res = run(inputs)
    x = np.asarray(inputs["x"])
    out = np.empty(x.shape, np.float32)
    SEG = x.shape[1] // 4
    for c in range(N_CORES):
        b, j = c // 4, c % 4
        out[b, j * SEG:(j + 1) * SEG] = res.results[c]["out"]
    return out
```
